# Optimizing a Trainium2 kernel written in Bass

```python
import jax, jax.numpy as jnp
from jax import lax
import numpy as np

D_MODEL = 2048
BATCH = 2
SEQ = 16384
DEPTH = 2
DEC_BATCH = 4
DEC_SEQ = 2048
PAST_LEN = 128

GRID_W = 64
NA_HEADS = 16
NA_HEAD_DIM = 64
D_ATTN = NA_HEADS * NA_HEAD_DIM
NA_KH_MAX = 8
NA_KW = 16
D_CONV = 1024
CONV_K = 31
D_FF = 5504
FFN_CONV_K = 3
N_BRANCH = 2
D_IN = 3 * D_ATTN + 2 * D_CONV + N_BRANCH * D_MODEL
N_MOD = 6
LN_EPS = 1e-5
DN_ALPHA = (2 * DEPTH) ** 0.25
DN_BETA = (8 * DEPTH) ** -0.25
NEG_INF = -1e9

kernel_name = 'hybrid_na_conformer_deepnorm_encoder'


def layer_norm(x, g, b):
    xf = x.astype(jnp.float32)
    mu = jnp.mean(xf, axis=-1, keepdims=True)
    var = jnp.mean(jnp.square(xf - mu), axis=-1, keepdims=True)
    y = ((xf - mu) * lax.rsqrt(var + LN_EPS)).astype(x.dtype)
    return y * g + b


def depthwise_conv(x, w, b):
    k = w.shape[0]
    y = lax.conv_general_dilated(
        x, w[:, None, :], window_strides=(1,), padding=[(k // 2, k // 2)],
        dimension_numbers=('NWC', 'WIO', 'NWC'), feature_group_count=x.shape[-1])
    return y + b


def neighbourhood_attention(q, k, v, rpb):
    B, T, H, Dh = q.shape
    rows = T // GRID_W
    kh = min(NA_KH_MAX, rows)
    qg = (q * (Dh ** -0.5)).reshape(B, rows, GRID_W, H, Dh)
    kg = k.reshape(B, rows, GRID_W, H, Dh)
    vg = v.reshape(B, rows, GRID_W, H, Dh)
    cols = jnp.arange(GRID_W)
    col_start = jnp.clip(cols - NA_KW // 2, 0, GRID_W - NA_KW)
    col_mask = (cols[None, :] >= col_start[:, None]) & (cols[None, :] < col_start[:, None] + NA_KW)
    dc_idx = jnp.clip(cols[None, :] - cols[:, None] + NA_KW - 1, 0, 2 * NA_KW - 2)
    rpb_cols = rpb[:, :, dc_idx]
    mask = col_mask[None, None, :, None, :]

    def one_row(r):
        r0 = jnp.clip(r - kh // 2, 0, rows - kh)
        k_blk = lax.dynamic_slice_in_dim(kg, r0, kh, axis=1)
        v_blk = lax.dynamic_slice_in_dim(vg, r0, kh, axis=1)
        q_row = lax.dynamic_index_in_dim(qg, r, axis=1, keepdims=False)
        dr_idx = r0 + jnp.arange(kh) - r + NA_KH_MAX - 1
        bias = jnp.take(rpb_cols, dr_idx, axis=1).transpose(0, 2, 1, 3)
        s = jnp.einsum('bqhd,brkhd->bhqrk', q_row, k_blk).astype(jnp.float32) + bias.astype(jnp.float32)
        s = jnp.where(mask, s, NEG_INF)
        p = jax.nn.softmax(s.reshape(B, H, GRID_W, kh * GRID_W), axis=-1)
        p = p.reshape(B, H, GRID_W, kh, GRID_W).astype(v.dtype)
        return jnp.einsum('bhqrk,brkhd->bqhd', p, v_blk)

    out = lax.map(one_row, jnp.arange(rows))
    return out.transpose(1, 0, 2, 3, 4).reshape(B, T, H * Dh)


def encoder_layer(x, c, w_mod, b_mod, w_in, b_in, na_rpb, w_attn_proj, conv_w, conv_b, conv_ln_g, conv_ln_b,
                  w_conv_proj, w_out, ln_mix_g, ln_mix_b, w_up, ffn_conv_w, ffn_conv_b, w_down,
                  ln_ffn_g, ln_ffn_b):
    B, T, D = x.shape
    mod = jnp.einsum('bd,de->be', jax.nn.silu(c), w_mod) + b_mod
    sh1, sc1, g1, sh2, sc2, g2 = jnp.split(mod[:, None, :], N_MOD, axis=-1)

    h = x * (1 + sc1) + sh1
    z = jnp.einsum('btd,de->bte', h, w_in) + b_in
    q, k, v, u, gates = jnp.split(
        z, [D_ATTN, 2 * D_ATTN, 3 * D_ATTN, 3 * D_ATTN + 2 * D_CONV], axis=-1)
    attn = neighbourhood_attention(q.reshape(B, T, NA_HEADS, NA_HEAD_DIM),
                                   k.reshape(B, T, NA_HEADS, NA_HEAD_DIM),
                                   v.reshape(B, T, NA_HEADS, NA_HEAD_DIM), na_rpb)
    y_a = jnp.einsum('bte,ed->btd', attn, w_attn_proj)
    u_val, u_gate = jnp.split(u, 2, axis=-1)
    u = u_val * jax.nn.sigmoid(u_gate)
    u = layer_norm(depthwise_conv(u, conv_w, conv_b), conv_ln_g, conv_ln_b)
    y_c = jnp.einsum('bte,ed->btd', jax.nn.silu(u), w_conv_proj)
    g_a, g_c = jnp.split(gates, N_BRANCH, axis=-1)
    y = jax.nn.sigmoid(g_a) * y_a + jax.nn.sigmoid(g_c) * y_c
    y = jnp.einsum('btd,de->bte', y, w_out)
    x = layer_norm(DN_ALPHA * x + g1 * y, ln_mix_g, ln_mix_b)

    h = x * (1 + sc2) + sh2
    up = jnp.einsum('btd,df->btf', h, w_up)
    a, b = jnp.split(up, 2, axis=-1)
    a = depthwise_conv(a, ffn_conv_w, ffn_conv_b)
    f = jnp.einsum('btf,fd->btd', jax.nn.gelu(a, approximate=False) * b, w_down)
    x = layer_norm(DN_ALPHA * x + g2 * f, ln_ffn_g, ln_ffn_b)
    return x


def trunk(x, c, ln_in_g, ln_in_b, w_mod, b_mod, w_in, b_in, na_rpb, w_attn_proj, conv_w, conv_b,
          conv_ln_g, conv_ln_b, w_conv_proj, w_out, ln_mix_g, ln_mix_b, w_up, ffn_conv_w, ffn_conv_b,
          w_down, ln_ffn_g, ln_ffn_b):
    x = layer_norm(x, ln_in_g, ln_in_b)
    for l in range(DEPTH):
        x = encoder_layer(x, c, w_mod[l], b_mod[l], w_in[l], b_in[l], na_rpb[l], w_attn_proj[l],
                          conv_w[l], conv_b[l], conv_ln_g[l], conv_ln_b[l], w_conv_proj[l], w_out[l],
                          ln_mix_g[l], ln_mix_b[l], w_up[l], ffn_conv_w[l], ffn_conv_b[l], w_down[l],
                          ln_ffn_g[l], ln_ffn_b[l])
    return x


def setup_inputs(seed: int = 0) -> dict:
    key = jax.random.key(seed)
    ks = jax.random.split(key, 26)
    L, D = DEPTH, D_MODEL

    def nrm(k, shape, scale):
        return jax.random.normal(k, shape, jnp.float32) * scale

    return {
        'x_prompt': nrm(ks[0], (BATCH, SEQ, D), 1.0),
        'x_sample': nrm(ks[1], (DEC_BATCH, DEC_SEQ, D), 1.0),
        'c_prompt': nrm(ks[2], (BATCH, D), 1.0),
        'c_sample': nrm(ks[3], (DEC_BATCH, D), 1.0),
        'ln_in_g': 1.0 + nrm(ks[4], (D,), 0.02),
        'ln_in_b': nrm(ks[5], (D,), 0.02),
        'w_mod': nrm(ks[6], (L, D, N_MOD * D), D ** -0.5),
        'b_mod': nrm(ks[7], (L, N_MOD * D), 0.02),
        'w_in': nrm(ks[8], (L, D, D_IN), D ** -0.5),
        'b_in': nrm(ks[9], (L, D_IN), 0.02),
        'na_rpb': nrm(ks[10], (L, NA_HEADS, 2 * NA_KH_MAX - 1, 2 * NA_KW - 1), 0.1),
        'w_attn_proj': nrm(ks[11], (L, D_ATTN, D), D_ATTN ** -0.5),
        'conv_w': nrm(ks[12], (L, CONV_K, D_CONV), CONV_K ** -0.5),
        'conv_b': nrm(ks[13], (L, D_CONV), 0.02),
        'conv_ln_g': 1.0 + nrm(ks[14], (L, D_CONV), 0.02),
        'conv_ln_b': nrm(ks[15], (L, D_CONV), 0.02),
        'w_conv_proj': nrm(ks[16], (L, D_CONV, D), D_CONV ** -0.5),
        'w_out': nrm(ks[17], (L, D, D), DN_BETA * D ** -0.5),
        'ln_mix_g': 1.0 + nrm(ks[18], (L, D), 0.02),
        'ln_mix_b': nrm(ks[19], (L, D), 0.02),
        'w_up': nrm(ks[20], (L, D, 2 * D_FF), D ** -0.5),
        'ffn_conv_w': nrm(ks[21], (L, FFN_CONV_K, D_FF), FFN_CONV_K ** -0.5),
        'ffn_conv_b': nrm(ks[22], (L, D_FF), 0.02),
        'w_down': nrm(ks[23], (L, D_FF, D), DN_BETA * D_FF ** -0.5),
        'ln_ffn_g': 1.0 + nrm(ks[24], (L, D), 0.02),
        'ln_ffn_b': nrm(ks[25], (L, D), 0.02),
    }


def reference(x_prompt, x_sample, c_prompt, c_sample, ln_in_g, ln_in_b, w_mod, b_mod, w_in, b_in, na_rpb,
              w_attn_proj, conv_w, conv_b, conv_ln_g, conv_ln_b, w_conv_proj, w_out, ln_mix_g, ln_mix_b,
              w_up, ffn_conv_w, ffn_conv_b, w_down, ln_ffn_g, ln_ffn_b):
    y_prompt = trunk(x_prompt, c_prompt, ln_in_g, ln_in_b, w_mod, b_mod, w_in, b_in, na_rpb, w_attn_proj,
                     conv_w, conv_b, conv_ln_g, conv_ln_b, w_conv_proj, w_out, ln_mix_g, ln_mix_b,
                     w_up, ffn_conv_w, ffn_conv_b, w_down, ln_ffn_g, ln_ffn_b)
    y_sample = trunk(x_sample, c_sample, ln_in_g, ln_in_b, w_mod, b_mod, w_in, b_in, na_rpb, w_attn_proj,
                     conv_w, conv_b, conv_ln_g, conv_ln_b, w_conv_proj, w_out, ln_mix_g, ln_mix_b,
                     w_up, ffn_conv_w, ffn_conv_b, w_down, ln_ffn_g, ln_ffn_b)
    return (y_prompt, y_sample)
```

```python
import contextlib
import numpy as np
import concourse.bass as bass
import concourse.mybir as mybir
from concourse.bass_utils import run_bass_kernel_spmd

F32 = mybir.dt.float32
BF16 = mybir.dt.bfloat16
AF = mybir.ActivationFunctionType
ALU = mybir.AluOpType

D = 2048
DEPTH = 2
NH = 16
DFF = 5504
NFC = 43
ALPHA = (2 * DEPTH) ** 0.25
EPS = 1e-5
NEG = -30000.0
HALO = 6
SEGS = [(0, 32), (44, 8)]
NBT = 64
NTOK = NBT * 128
SAME_SYNC = True


def rng_in(l, n):
    return [(0, n + 12), (3, n + 9)][l]


def rng_mix(l, n):
    return [(2, n + 10), (5, n + 7)][l]


def rng_ffn(l, n):
    return [(3, n + 9), (6, n + 6)][l]


def tiles(lo, hi, step):
    out = []
    b = lo
    while b < hi:
        nb = min(step, hi - b)
        out.append((b, nb))
        b += nb
    return out


class Tr:
    def __init__(self, nc, es, nsem=100):
        self.nc = nc
        self.eng = {'pe': nc.tensor, 'act': nc.scalar, 'dve': nc.vector, 'pool': nc.gpsimd, 'sp': nc.sync}
        self.sems = [es.enter_context(nc.semaphore(f"s{i}")) for i in range(nsem)]
        self.cnt = [0] * nsem
        self.esem = {e: i for i, e in enumerate(['pe', 'act', 'dve', 'pool'])}
        self.free = list(range(4, nsem))
        self.waited = {e: {} for e in self.eng}
        self.lastw = {}
        self.rd = {}
        self.nins = 0
        self.nwait = 0

    def alloc_sem(self):
        return self.free.pop()

    def free_sem(self, s):
        self.free.append(s)

    def _deps(self, reads, writes):
        d = {}
        for k in reads:
            t = self.lastw.get(k)
            if t is not None and d.get(t[0], 0) < t[1]:
                d[t[0]] = t[1]
        for k in writes:
            t = self.lastw.get(k)
            if t is not None and d.get(t[0], 0) < t[1]:
                d[t[0]] = t[1]
            r = self.rd.get(k)
            if r:
                for s, v in r.items():
                    if d.get(s, 0) < v:
                        d[s] = v
        return d

    def _wait(self, e, d):
        w = self.waited[e]
        own = self.esem.get(e)
        for s, v in d.items():
            if s >= 4:
                v = max(v, self.cnt[s])
            if s == own and (e == 'pe' or not SAME_SYNC):
                continue
            if w.get(s, 0) < v:
                self.eng[e].wait_ge(self.sems[s], v)
                w[s] = v
                self.nwait += 1

    def _record(self, tok, reads, writes):
        for k in writes:
            self.lastw[k] = tok
            self.rd[k] = {}
        for k in reads:
            r = self.rd.setdefault(k, {})
            if r.get(tok[0], 0) < tok[1]:
                r[tok[0]] = tok[1]

    def op(self, e, fn, reads=(), writes=()):
        self._wait(e, self._deps(reads, writes))
        ins = fn()
        s = self.esem[e]
        self.cnt[s] += 1
        ins.then_inc(self.sems[s], 1)
        self.nins += 1
        self._record((s, self.cnt[s]), reads, writes)

    def dma(self, q, out, in_, sem, reads=(), writes=()):
        self._wait(q, self._deps(reads, writes))
        ins = self.eng[q].dma_start(out=out, in_=in_)
        self.cnt[sem] += 16
        ins.then_inc(self.sems[sem], 16)
        self.nins += 1
        self._record((sem, self.cnt[sem]), reads, writes)

    def barrier(self, exclude=()):
        d = {s: c for s, c in enumerate(self.cnt) if c > 0 and s not in exclude}
        for e in self.eng:
            w = self.waited[e]
            for s, v in d.items():
                if w.get(s, 0) < v:
                    self.eng[e].wait_ge(self.sems[s], v)
                    w[s] = v
        self.lastw = {k: t for k, t in self.lastw.items() if t[0] in exclude}
        self.rd = {}


class Ring:
    uid = 0

    def __init__(self, tr, es, name, n, shape, dtype, dma=True):
        self.tr = tr
        self.name = name
        self.n = n
        Ring.uid += 1
        self.t = [es.enter_context(tr.nc.sbuf_tensor(f"{name}_{i}_u{Ring.uid}", shape, dtype)) for i in range(n)]
        self.sem = [tr.alloc_sem() if dma else None for _ in range(n)]
        self.i = -1
        es.callback(self._release)

    def _release(self):
        for s in self.sem:
            if s is not None:
                self.tr.free_sem(s)

    def next(self):
        self.i = (self.i + 1) % self.n
        return self.i


class Stream:
    def __init__(self, ring, items, depth, load_fn):
        self.ring, self.items, self.depth, self.load_fn = ring, list(items), depth, load_fn
        self.issued = 0
        self.taken = 0

    def get(self):
        while self.issued < min(len(self.items), self.taken + 1 + self.depth):
            self.load_fn(self.issued % self.ring.n, self.items[self.issued])
            self.issued += 1
        slot = self.taken % self.ring.n
        self.taken += 1
        return slot

    def prefetch(self):
        if self.issued < len(self.items) and self.issued <= self.taken:
            self.load_fn(self.issued % self.ring.n, self.items[self.issued])
            self.issued += 1


ST = 'act'


class Cols:
    def __init__(self):
        self.parts = []
        self.off = {}
        self.n = 0

    def add(self, name, arr):
        arr = np.ascontiguousarray(arr, dtype=np.float32).reshape(128, -1)
        self.off[name] = self.n
        self.n += arr.shape[1]
        self.parts.append(arr)

    def build(self):
        return np.ascontiguousarray(np.concatenate(self.parts, axis=1))


def col_layout():
    off = {}
    n = 0

    def add(name, w):
        nonlocal n
        off[name] = n
        n += w
    for l in range(DEPTH):
        add(f"b_in{l}", 72)
        add(f"conv_b{l}", 8)
        add(f"cln_g{l}", 8)
        add(f"cln_b{l}", 8)
        add(f"fcb{l}", NFC)
        add(f"fcw{l}", 3 * NFC)
        add(f"cw{l}", 8 * 31)
        add(f"bmod{l}", 64)
    add("cT", 32)
    add("valid", NBT)
    return off, n


COFF, NCOLS = col_layout()
ROW = {"ln_in_g": 0, "ln_in_b": 1}
for _l in range(DEPTH):
    for _i, _nm in enumerate(["mix_g", "mix_b", "ffn_g", "ffn_b", "bv", "bg1", "bg2"]):
        ROW[f"{_nm}{_l}"] = 2 + 7 * _l + _i
NROWS = 2 + 7 * DEPTH

def w_in_perm():
    q = np.arange(0, 1024)
    k = np.arange(1024, 2048)
    v = np.arange(2048, 3072)
    uval = np.arange(3072, 4096)
    ugate = np.arange(4096, 5120)
    gates = np.arange(5120, 9216)
    cols = [q, k, v]
    u = []
    for i in range(4):
        u.append(uval[256 * i:256 * (i + 1)])
        u.append(ugate[256 * i:256 * (i + 1)])
    cols += u
    cols.append(gates)
    return np.concatenate(cols)


W_IN_PERM = w_in_perm()


def tile_w(w, kc, cols_per_tile):
    K, N = w.shape
    assert K == kc * 128 and N % cols_per_tile == 0
    nt = N // cols_per_tile
    a = w.reshape(kc, 128, nt, cols_per_tile).transpose(2, 1, 0, 3)
    return np.ascontiguousarray(a).reshape(nt, 128, kc * cols_per_tile)


def host_weights(inp):
    out = {}
    wm, wi, wac, wo, wu, wd = [], [], [], [], [], []
    for l in range(DEPTH):
        w_mod = inp['w_mod'][l]
        perm = np.concatenate([np.arange(0, 2048), np.arange(2048, 4096), np.arange(6144, 8192),
                               np.arange(8192, 10240), np.arange(4096, 6144), np.arange(10240, 12288)])
        wm.append(tile_w(w_mod[:, perm], 16, 512))
        wi.append(tile_w(inp['w_in'][l][:, W_IN_PERM], 16, 512))
        wa = tile_w(inp['w_attn_proj'][l], 8, 512)
        wc = tile_w(inp['w_conv_proj'][l], 8, 512)
        wac.append(np.concatenate([wa, wc], axis=2))
        wo.append(tile_w(inp['w_out'][l], 16, 512))
        w_up = inp['w_up'][l]
        a = w_up[:, :DFF]
        b = w_up[:, DFF:]
        pad = np.zeros((D, 128), np.float32)
        cols = []
        for i in range(22):
            for src in (a, b):
                for r in range(2):
                    m = 2 * i + r
                    cols.append(src[:, 128 * m:128 * (m + 1)] if m < NFC else pad)
        wu.append(tile_w(np.concatenate(cols, axis=1), 16, 512))
        w_dn = inp['w_down'][l]
        w_dn = np.concatenate([w_dn, np.zeros((128, D), np.float32)], axis=0)
        t = []
        for c in range(4):
            for kg in range(4):
                blk = w_dn[kg * 11 * 128:(kg + 1) * 11 * 128, c * 512:(c + 1) * 512]
                t.append(tile_w(blk, 11, 512)[0])
        wd.append(np.stack(t))
    out['wf_mod'] = np.stack(wm)
    out['wf_in'] = np.stack(wi)
    out['wf_ac'] = np.stack(wac)
    out['wf_out'] = np.stack(wo)
    out['wf_up'] = np.stack(wu)
    out['wf_down'] = np.stack(wd)
    return out


WGROUPS = [("mod", 24, 8192), ("in", 18, 8192), ("ac", 4, 8192), ("out", 4, 8192), ("up", 22, 8192),
           ("down", 16, 5632)]


def chunkmajor(v):
    return np.ascontiguousarray(v.reshape(-1, 128).T)


def host_shared_tables(inp):
    c = Cols()
    for l in range(DEPTH):
        c.add(f"b_in{l}", chunkmajor(inp['b_in'][l][W_IN_PERM]))
        c.add(f"conv_b{l}", chunkmajor(inp['conv_b'][l]))
        c.add(f"cln_g{l}", chunkmajor(inp['conv_ln_g'][l]))
        c.add(f"cln_b{l}", chunkmajor(inp['conv_ln_b'][l]))
        c.add(f"fcb{l}", chunkmajor(inp['ffn_conv_b'][l]))
        fw = inp['ffn_conv_w'][l]
        c.add(f"fcw{l}", np.concatenate([chunkmajor(fw[j]) for j in range(3)], axis=1))
        cw = inp['conv_w'][l]
        c.add(f"cw{l}", cw.T.reshape(8, 128, 31).transpose(1, 0, 2).reshape(128, 248))
        bm = inp['b_mod'][l]
        c.add(f"bmod{l}", chunkmajor(np.concatenate([bm[0:2048], bm[2048:4096], bm[6144:8192], bm[8192:10240]])))
    shared = c
    rows = np.zeros((NROWS, D), np.float32)
    rows[ROW["ln_in_g"]] = inp['ln_in_g']
    rows[ROW["ln_in_b"]] = inp['ln_in_b']
    for l in range(DEPTH):
        rows[ROW[f"mix_g{l}"]] = inp['ln_mix_g'][l]
        rows[ROW[f"mix_b{l}"]] = inp['ln_mix_b'][l]
        rows[ROW[f"ffn_g{l}"]] = inp['ln_ffn_g'][l]
        rows[ROW[f"ffn_b{l}"]] = inp['ln_ffn_b'][l]
        rows[ROW[f"bv{l}"], :1024] = inp['b_in'][l][2048:3072]
        rows[ROW[f"bg1{l}"]] = inp['b_mod'][l][4096:6144]
        rows[ROW[f"bg2{l}"]] = inp['b_mod'][l][10240:12288]
    rpb = inp['na_rpb']
    kc = np.arange(64)[:, None]
    qc = np.arange(64)[None, :]
    cs = np.clip(qc - 8, 0, 48)
    colmask = (kc >= cs) & (kc < cs + 16)
    dci = np.clip(kc - qc + 15, 0, 30)
    U = np.full((DEPTH, 2, 64, NH, 7, 2, 64), NEG, np.float32)
    for a in range(2):
        for sdx in range(7):
            for b in range(2):
                dr = 2 * (sdx - 3) + a - b
                if -7 <= dr <= 7:
                    blk = rpb[:, :, dr + 7, :][:, :, dci]
                    blk = np.where(colmask[None, None], blk, np.float32(NEG))
                    U[:, a, :, :, sdx, b, :] = blk.transpose(0, 2, 1, 3)
    U = U.reshape(DEPTH, 128, NH * 896)
    kp = np.zeros((16, 8, 128), np.float32)
    for b in range(8):
        for a in range(2):
            kp[(2 * b + a) % 16, b, a * 64:(a + 1) * 64] = 1.0
    ident = np.eye(128, dtype=np.float32)
    sel = np.zeros((3, 2, 128), np.float32)
    sel[0, 0] = 1.0
    sel[1, 1] = 1.0
    sel[2, :] = 1.0
    return shared, rows, np.ascontiguousarray(U), kp.reshape(16, 1024), ident, sel.reshape(3, 256)


def core_geometry(core):
    pj, sj = core % 4, core % 2
    real = np.full(NBT, -1, np.int64)
    nblk = np.zeros(NBT, np.int64)
    for vb in range(44):
        rb = 32 * pj - HALO + vb
        nblk[vb] = 128
        if 0 <= rb < 128:
            real[vb] = rb
    for vb in range(20):
        rb = 8 * sj - HALO + vb
        nblk[44 + vb] = 16
        if 0 <= rb < 16:
            real[44 + vb] = rb
    return real, nblk


def host_core_tables(inp, core, shared_cols):
    pb, sb = core // 4, core // 2
    real, nblk = core_geometry(core)
    xin = np.zeros((NTOK, D), np.float32)
    for g in range(NBT):
        if real[g] >= 0:
            src = inp['x_prompt'][pb] if g < 44 else inp['x_sample'][sb]
            xin[g * 128:(g + 1) * 128] = src[real[g] * 128:(real[g] + 1) * 128]
    c = Cols()
    c.parts = list(shared_cols.parts)
    c.off = dict(shared_cols.off)
    c.n = shared_cols.n
    cpair = np.stack([inp['c_prompt'][pb], inp['c_sample'][sb]], axis=1)
    c.add("cT", cpair.reshape(16, 128, 2).transpose(1, 0, 2).reshape(128, 32))
    valid = (real >= 0).astype(np.float32)
    c.add("valid", np.broadcast_to(valid[None, :], (128, NBT)))
    assert c.off == COFF and c.n == NCOLS
    qm = np.zeros((16, NTOK), np.float32)
    for g in range(NBT):
        base = 0 if g < 44 else 44
        nseg = 44 if g < 44 else 20
        for a in range(2):
            cols = slice(g * 128 + a * 64, g * 128 + (a + 1) * 64)
            if real[g] < 0:
                continue
            rows_total = nblk[g] * 2
            r = real[g] * 2 + a
            r0 = min(max(r - 4, 0), rows_total - 8)
            qv = 2 * g + a
            for j in range(16):
                ok = False
                for kv in range(qv - 7, qv + 9):
                    if kv % 16 != j:
                        continue
                    kg = kv // 2
                    if kg < base or kg >= base + nseg or real[kg] < 0:
                        continue
                    kr = real[kg] * 2 + (kv % 2)
                    if r0 <= kr < r0 + 8:
                        ok = True
                qm[j, cols] = 0.0 if ok else NEG
    return xin, c.build(), qm


def build(stop_after=None, debug_outs=()):
    nc = bass.Bass("TRN2", target_bir_lowering=False)

    def dram(name, shape, dt, kind="Internal"):
        if name in debug_outs:
            kind = "ExternalOutput"
        return nc.dram_tensor(name, shape, dt, kind=kind).ap()

    xin = dram("xin", [NTOK, D], F32, "ExternalInput")
    vecs_d = dram("vecs", [128, NCOLS], F32, "ExternalInput")
    rows_d = dram("rows", [NROWS, D], F32, "ExternalInput")
    utab_d = dram("utab", [DEPTH, 128, NH * 896], F32, "ExternalInput")
    qmask_d = dram("qmask", [16, NTOK], F32, "ExternalInput")
    kpat_d = dram("kpat", [16, 1024], F32, "ExternalInput")
    ident_d = dram("ident", [128, 128], F32, "ExternalInput")
    sel_d = dram("sel", [3, 256], F32, "ExternalInput")
    wf = {g: dram(f"wf_{g}", [DEPTH, nt, 128, w], F32, "ExternalInput") for g, nt, w in WGROUPS}
    ws = {g: dram(f"ws_{g}", [DEPTH, nt, 128, w], BF16) for g, nt, w in WGROUPS}
    yout = dram("yout", [40 * 128, D], F32, "ExternalOutput")

    X0 = dram("X0", [NTOK, D], F32)
    XM = dram("XM", [NTOK, D], F32)
    X1 = dram("X1", [NTOK, D], F32)
    H1T = dram("H1T", [128, 16, NTOK + 2], BF16)
    H2T = dram("H2T", [128, 16, NTOK + 2], BF16)
    QT = dram("QT", [NBT, 128, 1024], BF16)
    KT = dram("KT", [NBT, 128, 1024], BF16)
    VV = dram("VV", [NBT, 128, 1040], BF16)
    UT = dram("UT", [128, 8, NTOK + 32], BF16)
    GT = dram("GT", [128, 32, NTOK], BF16)
    AT = dram("AT", [128, 8, NTOK], BF16)
    CT = dram("CT", [128, 8, NTOK], BF16)
    GBS = dram("GBS", [DEPTH * 4, 128, D], F32)

    es = contextlib.ExitStack()
    with es:
        tr = Tr(nc, es)
        ps = [es.enter_context(nc.psum_tensor(f"ps{i}", [128, 512], F32)) for i in range(8)]
        psk = [("ps", i) for i in range(8)]

        def sb(stack, name, shape, dt):
            Ring.uid += 1
            return stack.enter_context(nc.sbuf_tensor(f"{name}_u{Ring.uid}", shape, dt))

        vecs = sb(es, "vecs_sb", [128, NCOLS], F32)
        identb = sb(es, "identb", [128, 128], BF16)
        onesrow = sb(es, "onesrow", [1, 128], BF16)
        modT = sb(es, "modT", [128, DEPTH, 64, 2], F32)
        csem = tr.alloc_sem()
        tr.dma('sp', vecs[:], vecs_d[:, :], csem, writes=["vecs"])
        tr.dma('pool', identb[:], ident_d[:, :], csem, writes=["identb"])
        tr.op('dve', lambda: nc.vector.memset(onesrow[:], 1.0), writes=["onesrow"])

        def vcol(name, i=0, w=1):
            o = COFF[name] + i
            return vecs[:, o:o + w]

        cast_sems = set()
        wkeys = {}
        order = [(0, "mod"), (1, "mod")]
        for l in range(DEPTH):
            order += [(l, g) for g in ["in", "ac", "out", "up", "down"]]
        gdims = {g: (nt, w) for g, nt, w in WGROUPS}
        for l, g in order:
            nt, w = gdims[g]
            s = tr.alloc_sem()
            cast_sems.add(s)
            keys = []
            for t0 in range(0, nt, 2):
                t1 = min(nt, t0 + 2)
                k = ("ws", g, l, t0)
                keys.append(k)
                tr.dma('pool', ws[g][l, t0:t1].rearrange("t p c -> p t c"),
                       wf[g][l, t0:t1].rearrange("t p c -> p t c"), s, writes=[k])
            wkeys[(g, l)] = keys

        def phase_end():
            tr.barrier(exclude=cast_sems)

        class Epi:
            def __init__(self, st, rowg, rowb):
                self.xb = Ring(tr, st, "e_xb", 2, [128, D], BF16, dma=False)
                self.hts = Ring(tr, st, "e_hts", 3, [128, D], BF16)
                self.stt = sb(st, "e_st", [128, 4, 6], F32)
                self.mv = sb(st, "e_mv", [128, 8], F32)
                self.tv = Ring(tr, st, "e_tv", 2, [128, 32], F32, dma=False)
                self.gB = sb(st, "e_gB", [128, D], F32)
                self.bB = sb(st, "e_bB", [128, D], F32)
                self.sem = tr.alloc_sem()
                st.callback(lambda: tr.free_sem(self.sem))
                tr.dma('sp', self.gB[:], rows_d[rowg:rowg + 1, :].partition_broadcast(128), self.sem, writes=["e_gB"])
                tr.dma('sp', self.bB[:], rows_d[rowb:rowb + 1, :].partition_broadcast(128), self.sem, writes=["e_bB"])

            def run(self, xp, kxp, xsem, g, xdst, hdst, S=None, T=None, xkey=None, hkey=None):
                stt, mv = self.stt, self.mv
                for i in range(4):
                    tr.op('dve', lambda i=i: nc.vector.bn_stats(out=stt[:, i, :], in_=xp[:, i * 512:(i + 1) * 512]),
                          reads=[kxp], writes=[("e_st", i)])
                tr.op('dve', lambda: nc.vector.bn_aggr(out=mv[:, 0:2], in_=stt[:]),
                      reads=[("e_st", i) for i in range(4)], writes=["e_mv"])
                tr.op('act', lambda: nc.scalar.activation(out=mv[:, 2:3], in_=mv[:, 1:2], func=AF.Sqrt, bias=EPS, scale=1.0),
                      reads=["e_mv"], writes=["e_sd"])
                tr.op('dve', lambda: nc.vector.reciprocal(out=mv[:, 3:4], in_=mv[:, 2:3]), reads=["e_sd"], writes=["e_rstd"])
                tr.op('dve', lambda: nc.vector.scalar_tensor_tensor(out=mv[:, 4:5], in0=mv[:, 0:1], scalar=-1.0, in1=mv[:, 3:4],
                                                                    op0=ALU.mult, op1=ALU.mult),
                      reads=["e_mv", "e_rstd"], writes=["e_nmr"])
                tr.op('act', lambda: nc.scalar.activation(out=xp, in_=xp, func=AF.Identity, bias=mv[:, 4:5], scale=mv[:, 3:4]),
                      reads=[kxp, "e_rstd", "e_nmr"], writes=[kxp])
                tr.op('dve', lambda: nc.vector.tensor_tensor(out=xp, in0=xp, in1=self.gB[:], op=ALU.mult),
                      reads=[kxp, "e_gB"], writes=[kxp])
                tr.op('dve', lambda: nc.vector.tensor_tensor(out=xp, in0=xp, in1=self.bB[:], op=ALU.add),
                      reads=[kxp, "e_bB"], writes=[kxp])
                if xdst is not None:
                    tr.dma(ST, xdst, xp, xsem, reads=[kxp], writes=[xkey] if xkey else [])
                if hdst is None:
                    return
                rb = self.xb.next()
                xb = self.xb.t[rb]
                tr.op('act', lambda: nc.scalar.copy(out=xb[:], in_=xp), reads=[kxp], writes=[("e_xb", rb)])
                pA = ps[6][:].bitcast(BF16)
                pB = ps[7][:].bitcast(BF16)

                def tp():
                    for k in range(16):
                        dst = (pA if k < 8 else pB)[:, (k % 8) * 128:(k % 8 + 1) * 128]
                        ins = nc.tensor.transpose(dst, xb[:, k * 128:(k + 1) * 128], identb[:])
                    return ins
                tr.op('pe', tp, reads=[("e_xb", rb), "identb"], writes=[psk[6], psk[7]])
                vc = vcol("valid", g)
                ti = self.tv.next()
                tv = self.tv.t[ti]
                tr.op('dve', lambda: nc.vector.tensor_scalar(out=tv[:, 0:16], in0=S, scalar1=vc, scalar2=None, op0=ALU.mult),
                      reads=["vecs", "modT"], writes=[("e_tv", ti, 0)])
                tr.op('dve', lambda: nc.vector.tensor_scalar(out=tv[:, 16:32], in0=T, scalar1=vc, scalar2=None, op0=ALU.mult),
                      reads=["vecs", "modT"], writes=[("e_tv", ti, 1)])
                rh = self.hts.next()
                hts = self.hts.t[rh]
                for k in range(16):
                    pp = pA if k < 8 else pB
                    tr.op('act', lambda k=k, pp=pp: nc.scalar.activation(
                        out=hts[:, k * 128:(k + 1) * 128], in_=pp[:, (k % 8) * 128:(k % 8 + 1) * 128], func=AF.Identity,
                        bias=tv[:, 16 + k:17 + k], scale=tv[:, k:k + 1]),
                        reads=[psk[6 + k // 8], ("e_tv", ti, 0), ("e_tv", ti, 1)], writes=[("e_hts", rh, k)])
                tr.dma(ST, hdst, hts[:].rearrange("p (k t) -> p k t", t=128), self.hts.sem[rh],
                       reads=[("e_hts", rh, k) for k in range(16)], writes=[hkey] if hkey else [])

        def S_of(l, which, seg):
            return modT[:, l, 16 + 32 * which:32 + 32 * which, seg]

        def T_of(l, which, seg):
            return modT[:, l, 32 * which:16 + 32 * which, seg]

        def seg_of(g):
            return 0 if g < 44 else 1

        def phase_modpre():
            with contextlib.ExitStack() as st:
                W = Ring(tr, st, "m_w", 3, [128, 8192], BF16)
                scT = sb(st, "m_scT", [128, 32], BF16)
                G3 = Ring(tr, st, "m_g3", 2, [3, 512], F32)
                gbo = Ring(tr, st, "m_gbo", 2, [128, 512], F32)
                self_sel = sb(st, "m_sel", [3, 256], F32)
                msem = tr.alloc_sem()
                tr.dma('sp', self_sel[:], sel_d[:, :], msem, writes=["m_sel"])
                tr.op('act', lambda: nc.scalar.activation(out=scT[:], in_=vcol("cT", 0, 32), func=AF.Silu),
                      reads=["vecs"], writes=["m_scT"])
                wst = Stream(W, [(l, i) for l in range(DEPTH) for i in range(24)], 2,
                             lambda slot, it: tr.dma('sp', W.t[slot][:], ws["mod"][it[0], it[1]], W.sem[slot],
                                                     reads=wkeys[("mod", it[0])], writes=[("m_w", slot)]))
                epi = Epi(st, ROW["ln_in_g"], ROW["ln_in_b"])
                xr = Ring(tr, st, "p_x", 3, [128, D], F32)
                blocks = [base + vb for base, n in SEGS for vb in range(*rng_in(0, n))]
                xs = Stream(xr, blocks, 2, lambda slot, g: tr.dma('sp', xr.t[slot][:], xin[g * 128:(g + 1) * 128, :],
                                                                 xr.sem[slot], writes=[("p_x", slot)]))

                def mod_layer(l):
                    pm = ps[0]
                    for i in range(16):
                        w = wst.get()
                        wt = W.t[w]
                        wv = wt[:].rearrange("p (k c) -> p k c", c=512)

                        def mm(i=i, wv=wv):
                            for m in range(4):
                                n = 4 * i + m
                                for k in range(16):
                                    ins = nc.tensor.matmul(pm[:, 2 * n:2 * n + 2], lhsT=wv[:, k, m * 128:(m + 1) * 128],
                                                           rhs=scT[:, 2 * k:2 * k + 2], start=(k == 0), stop=(k == 15))
                            return ins
                        tr.op('pe', mm, reads=[("m_w", w), "m_scT"], writes=[psk[0]])
                        yield
                    tr.op('dve', lambda l=l: nc.vector.tensor_tensor(
                        out=modT[:, l], in0=pm[:, 0:128].rearrange("p (n b) -> p n b", b=2),
                        in1=vcol(f"bmod{l}", 0, 64).unsqueeze(2).to_broadcast([128, 64, 2]), op=ALU.add),
                        reads=[psk[0], "vecs"], writes=["modT"])
                    for wh in range(2):
                        tr.op('dve', lambda l=l, wh=wh: nc.vector.tensor_scalar(
                            out=modT[:, l, 16 + 32 * wh:32 + 32 * wh, :], in0=modT[:, l, 16 + 32 * wh:32 + 32 * wh, :],
                            scalar1=1.0, scalar2=None, op0=ALU.add), reads=["modT"], writes=["modT"])
                    for i in range(8):
                        wh, cg = i // 4, i % 4
                        w = wst.get()
                        wt = W.t[w]
                        wv = wt[:].rearrange("p (k c) -> p k c", c=512)
                        pg = ps[1 + (i % 2)]

                        def mm2(wv=wv, pg=pg):
                            for k in range(16):
                                ins = nc.tensor.matmul(pg[0:2, :], lhsT=scT[:, 2 * k:2 * k + 2], rhs=wv[:, k, :],
                                                       start=(k == 0), stop=(k == 15))
                            return ins
                        tr.op('pe', mm2, reads=[("m_w", w), "m_scT"], writes=[psk[1 + (i % 2)]])
                        gi = G3.next()
                        g3 = G3.t[gi]
                        tr.op('act', lambda g3=g3, pg=pg: nc.scalar.copy(out=g3[0:2, :], in_=pg[0:2, :]),
                              reads=[psk[1 + (i % 2)]], writes=[("m_g3", gi, 0)])
                        rr = ROW[f"bg{wh + 1}{l}"]
                        tr.dma('sp', g3[2:3, :], rows_d[rr:rr + 1, cg * 512:(cg + 1) * 512], G3.sem[gi],
                               writes=[("m_g3", gi, 1)])
                        for seg in range(2):
                            pq = ps[3 + seg]
                            tr.op('pe', lambda pq=pq, g3=g3, seg=seg: nc.tensor.matmul(
                                pq[:, :], lhsT=self_sel[:, seg * 128:(seg + 1) * 128], rhs=g3[:, :], start=True, stop=True),
                                reads=[("m_g3", gi, 0), ("m_g3", gi, 1), "m_sel"], writes=[psk[3 + seg]])
                            oi = gbo.next()
                            tr.op('act', lambda oi=oi, pq=pq: nc.scalar.copy(out=gbo.t[oi][:], in_=pq[:, :]),
                                  reads=[psk[3 + seg]], writes=[("m_gbo", oi)])
                            tr.dma(ST, GBS[l * 4 + wh * 2 + seg, :, cg * 512:(cg + 1) * 512], gbo.t[oi][:], gbo.sem[oi],
                                   reads=[("m_gbo", oi)], writes=[("GBS", l, wh, seg, cg)])
                        yield
                for _ in mod_layer(0):
                    pass
                g1 = mod_layer(1)
                for bi, g in enumerate(blocks):
                    seg = seg_of(g)
                    r = xs.get()
                    epi.run(xr.t[r][:], ("p_x", r), xr.sem[r], g, X0[g * 128:(g + 1) * 128, :],
                            H1T[:, :, 1 + g * 128:1 + (g + 1) * 128], S=S_of(0, 0, seg), T=T_of(0, 0, seg),
                            xkey=("X0", g), hkey=("H1T", g))
                    if bi % 2 == 1:
                        next(g1, None)
                for _ in g1:
                    pass
                phase_end()
                tr.free_sem(msem)

        def phase_p1(l):
            with contextlib.ExitStack() as st:
                W = Ring(tr, st, "a_w", 3, [128, 8192], BF16)
                HT = Ring(tr, st, "a_h", 2, [128, 16 * 512], BF16)
                SG = Ring(tr, st, "a_sg", 4, [128, 2048], BF16)
                VS = Ring(tr, st, "a_vs", 2, [128, 4 * 520], BF16)
                SIG = Ring(tr, st, "a_sig", 2, [128, 512], F32, dma=False)
                bvf = sb(st, "a_bvf", [1, 1024], F32)
                bvb = sb(st, "a_bvb", [1, 1024], BF16)
                s0 = tr.alloc_sem()
                rr = ROW[f"bv{l}"]
                tr.dma('sp', bvf[:], rows_d[rr:rr + 1, 0:1024], s0, writes=["a_bvf"])
                tr.op('act', lambda: nc.scalar.copy(out=bvb[:], in_=bvf[:]), reads=["a_bvf"], writes=["a_bvb"])
                for i in range(2):
                    tr.op('dve', lambda i=i: nc.vector.memset(VS.t[i][:], 1.0), writes=[("a_vs", i)])
                pi = [0]

                def bank():
                    pi[0] = (pi[0] + 1) % 6
                    return pi[0]
                tl = [(base + vb0, nb) for base, n in SEGS for vb0, nb in tiles(*rng_in(l, n), 4)]

                def load_h(slot, it):
                    g0_, nb_ = it
                    tr.dma('sp', HT.t[slot][:].rearrange("p (k t) -> p k t", t=512)[:, :, 0:nb_ * 128],
                           H1T[:, :, 1 + g0_ * 128:1 + (g0_ + nb_) * 128], HT.sem[slot],
                           reads=[("H1T", g0_ + j) for j in range(nb_)], writes=[("a_h", slot)])
                hst = Stream(HT, tl, 1, load_h)

                def sub_range(g0_, nb_):
                    base_, n_ = SEGS[0] if g0_ < 44 else SEGS[1]
                    lo_m, hi_m = rng_mix(l, n_)
                    return max(0, base_ + lo_m - g0_), min(nb_, base_ + hi_m - g0_)

                def tile_list(g0_, nb_):
                    ja_, jb_ = sub_range(g0_, nb_)
                    return [i for i in range(18) if jb_ > ja_ or (2 <= i < 10)]
                wst = Stream(W, [i for g0_, nb_ in tl for i in tile_list(g0_, nb_)], 2,
                             lambda slot, i: tr.dma('sp', W.t[slot][:], ws["in"][l, i], W.sem[slot],
                                                    reads=wkeys[("in", l)], writes=[("a_w", slot)]))
                if True:
                    for g0, nb in tl:
                        N = nb * 128
                        t0 = g0 * 128
                        hr = hst.get()
                        hT = HT.t[hr][:].rearrange("p (k t) -> p k t", t=512)
                        kh = ("a_h", hr)
                        ja, jb = sub_range(g0, nb)
                        for i in tile_list(g0, nb):
                            w = wst.get()
                            wv = W.t[w][:].rearrange("p (k c) -> p k c", c=512)
                            kw = ("a_w", w)

                            def fm(pb, m, wv=wv, hT=hT, N=N, c0=0):
                                def f():
                                    for k in range(16):
                                        ins = nc.tensor.matmul(ps[pb][:, 0:N], lhsT=wv[:, k, m * 128:(m + 1) * 128],
                                                               rhs=hT[:, k, c0:c0 + N], start=(k == 0), stop=(k == 15))
                                    return ins
                                tr.op('pe', f, reads=[kw, kh], writes=[psk[pb]])
                            if i < 4:
                                si = SG.next()
                                sg = SG.t[si]
                                qa, qb = (ja, jb) if i < 2 else (0, nb)
                                nq = qb - qa
                                for m in range(4):
                                    pb = bank()
                                    fm(pb, m, N=nq * 128, c0=qa * 128)
                                    o = sg[:, 0:nq * 512].rearrange("p (b m t) -> p b m t", m=4, t=128)[:, :, m, :]
                                    src = ps[pb][:, 0:nq * 128].rearrange("p (b t) -> p b t", t=128)
                                    bc = vcol(f"b_in{l}", 4 * i + m)
                                    if i < 2:
                                        tr.op('dve', lambda o=o, src=src, bc=bc: nc.vector.tensor_scalar(
                                            out=o, in0=src, scalar1=bc, scalar2=0.125, op0=ALU.add, op1=ALU.mult),
                                            reads=[psk[pb], "vecs"], writes=[("a_sg", si, m)])
                                    else:
                                        tr.op('act', lambda o=o, src=src, bc=bc: nc.scalar.activation(
                                            out=o, in_=src, func=AF.Identity, bias=bc, scale=1.0),
                                            reads=[psk[pb], "vecs"], writes=[("a_sg", si, m)])
                                dst = (QT if i < 2 else KT)[g0 + qa:g0 + qb, :, (i % 2) * 512:(i % 2 + 1) * 512]
                                tr.dma(ST, dst.rearrange("b p c -> p b c"),
                                       sg[:, 0:nq * 512].rearrange("p (b c) -> p b c", c=512), SG.sem[si],
                                       reads=[("a_sg", si, m) for m in range(4)],
                                       writes=[("QT" if i < 2 else "KT", g0 + j, i % 2) for j in range(qa, qb)])
                            elif i < 6:
                                vi = VS.next()
                                vs = VS.t[vi]
                                hh = i - 4
                                for j in range(nb):
                                    pb = bank()

                                    def f(pb=pb, j=j, wv=wv, hT=hT):
                                        for k in range(16):
                                            nc.tensor.matmul(ps[pb][:, :], lhsT=hT[:, k, j * 128:(j + 1) * 128], rhs=wv[:, k, :],
                                                             start=(k == 0), stop=False)
                                        return nc.tensor.matmul(ps[pb][:, :], lhsT=onesrow[0:1, :],
                                                                rhs=bvb[0:1, hh * 512:(hh + 1) * 512], start=False, stop=True)
                                    tr.op('pe', f, reads=[kw, kh, "a_bvb", "onesrow"], writes=[psk[pb]])
                                    o = vs[:, j * 520:(j + 1) * 520].rearrange("p (h e) -> p h e", e=65)[:, :, 0:64]
                                    tr.op('act', lambda o=o, pb=pb: nc.scalar.copy(
                                        out=o, in_=ps[pb][:, :].rearrange("p (h e) -> p h e", e=64)),
                                        reads=[psk[pb]], writes=[("a_vs", vi, j)])
                                tr.dma(ST, VV[g0:g0 + nb, :, hh * 520:(hh + 1) * 520].rearrange("b p c -> p b c"),
                                       vs[:, 0:nb * 520].rearrange("p (b c) -> p b c", c=520), VS.sem[vi],
                                       reads=[("a_vs", vi, j) for j in range(nb)] + [("a_vs", vi)],
                                       writes=[("VV", g0 + j, hh) for j in range(nb)])
                            elif i < 10:
                                si = SG.next()
                                sg = SG.t[si]
                                ii = i - 6
                                for r in range(2):
                                    pv = bank()
                                    fm(pv, r)
                                    pg = bank()
                                    fm(pg, 2 + r)
                                    gi = SIG.next()
                                    sig = SIG.t[gi]
                                    bg = vcol(f"b_in{l}", 4 * i + 2 + r)
                                    bvv = vcol(f"b_in{l}", 4 * i + r)
                                    tr.op('act', lambda sig=sig, pg=pg, bg=bg: nc.scalar.activation(
                                        out=sig[:, 0:N], in_=ps[pg][:, 0:N], func=AF.Sigmoid, bias=bg, scale=1.0),
                                        reads=[psk[pg], "vecs"], writes=[("a_sig", gi)])
                                    tr.op('dve', lambda sig=sig, pv=pv, bvv=bvv, r=r, sg=sg: nc.vector.scalar_tensor_tensor(
                                        out=sg[:, r * N:(r + 1) * N], in0=ps[pv][:, 0:N], scalar=bvv, in1=sig[:, 0:N],
                                        op0=ALU.add, op1=ALU.mult),
                                        reads=[psk[pv], ("a_sig", gi), "vecs"], writes=[("a_sg", si, r)])
                                tr.dma(ST, UT[:, 2 * ii:2 * ii + 2, 16 + t0:16 + t0 + N],
                                       sg[:, 0:2 * N].rearrange("p (m t) -> p m t", t=N), SG.sem[si],
                                       reads=[("a_sg", si, r) for r in range(2)],
                                       writes=[("UT", g0 + j, ii) for j in range(nb)])
                            else:
                                si = SG.next()
                                sg = SG.t[si]
                                ii = i - 10
                                Ng = (jb - ja) * 128
                                for m in range(4):
                                    pb = bank()
                                    fm(pb, m, N=Ng, c0=ja * 128)
                                    bc = vcol(f"b_in{l}", 4 * i + m)
                                    tr.op('act', lambda sg=sg, pb=pb, bc=bc, m=m: nc.scalar.activation(
                                        out=sg[:, m * Ng:(m + 1) * Ng], in_=ps[pb][:, 0:Ng], func=AF.Sigmoid, bias=bc, scale=1.0),
                                        reads=[psk[pb], "vecs"], writes=[("a_sg", si, m)])
                                pos = 8 * ii if ii < 4 else 8 * (ii - 4) + 4
                                tr.dma(ST, GT[:, pos:pos + 4, t0 + ja * 128:t0 + jb * 128],
                                       sg[:, 0:4 * Ng].rearrange("p (m t) -> p m t", t=Ng), SG.sem[si],
                                       reads=[("a_sg", si, m) for m in range(4)],
                                       writes=[("GT", g0 + j, pos) for j in range(ja, jb)])
                phase_end()
                tr.free_sem(s0)

        def phase_p2(l):
            from collections import deque
            with contextlib.ExitStack() as st:
                U = sb(st, "b_U", [128, NH * 896], BF16)
                Qm = sb(st, "b_qm", [80, NTOK], BF16)
                Kp = sb(st, "b_kp", [80, 1024], BF16)
                s0 = tr.alloc_sem()
                tr.dma('pool', U[:], utab_d[l], s0, writes=["b_U"])
                for pofs in (0, 64):
                    tr.dma('pool', Qm[pofs:pofs + 16, :], qmask_d[:, :], s0, writes=[("b_qm", pofs)])
                    tr.dma('pool', Kp[pofs:pofs + 16, :], kpat_d[:, :], s0, writes=[("b_kp", pofs)])
                for piece in range(7):
                    tr.op('act', lambda piece=piece: nc.scalar.activation(
                        out=U[:, piece * 2048:(piece + 1) * 2048], in_=U[:, piece * 2048:(piece + 1) * 2048], func=AF.Exp),
                        reads=["b_U"], writes=["b_U"])
                Uv = U[:].rearrange("p (h s c) -> p h s c", s=7, c=128)
                XR = Ring(tr, st, "b_xr", 4, [128, 512], BF16, dma=False)
                Qr = Ring(tr, st, "b_q", 3, [128, 1024], BF16)
                Kr = Ring(tr, st, "b_k", 8, [128, 1024], BF16)
                Vr = Ring(tr, st, "b_v", 8, [128, 1040], BF16)
                PT = Ring(tr, st, "b_pt", 4, [128, 512], BF16, dma=False)
                AO = Ring(tr, st, "b_ao", 2, [128, 1024], BF16, dma=False)
                ATS = Ring(tr, st, "b_ats", 2, [128, 1024], BF16)
                RC = Ring(tr, st, "b_rc", 2, [128, 4], F32, dma=False)
                LOOK = 4
                SB = [0, 1, 2, 6, 7]
                scnt = [0]
                mcnt = [0]
                for base, n in SEGS:
                    lo, hi = rng_mix(l, n)
                    loaded = set()
                    items = []
                    for vb in range(lo, hi):
                        g = base + vb
                        cs = list(range(g - 2, g + 3))
                        if vb == HALO:
                            cs.append(g + 3)
                        if vb == HALO + n - 1:
                            cs.insert(0, g - 3)
                        bst = {"g": g, "cs": cs, "edge": vb in (HALO, HALO + n - 1)}
                        for h in range(NH):
                            for gi_, grp in enumerate((cs[0:4], cs[4:])):
                                items.append({"b": bst, "h": h, "gi": gi_, "grp": grp})

                    def start(it):
                        bst = it["b"]
                        g, cs, h, grp = bst["g"], bst["cs"], it["h"], it["grp"]
                        if h == 0 and it["gi"] == 0:
                            for c in cs:
                                if c not in loaded:
                                    loaded.add(c)
                                    tr.dma('sp', Kr.t[c % 8][:], KT[c], Kr.sem[c % 8], reads=[("KT", c, 0), ("KT", c, 1)],
                                           writes=[("b_k", c % 8)])
                                    tr.dma('sp', Vr.t[c % 8][:], VV[c], Vr.sem[c % 8], reads=[("VV", c, 0), ("VV", c, 1)],
                                           writes=[("b_v", c % 8)])
                            qi = Qr.next()
                            tr.dma('sp', Qr.t[qi][:], QT[g], Qr.sem[qi], reads=[("QT", g, 0), ("QT", g, 1)], writes=[("b_q", qi)])
                            bst["qi"] = qi
                            bst["ai"] = AO.next()
                        qi = bst["qi"]
                        q = Qr.t[qi]
                        t, p0 = h // 2, 64 * (h % 2)
                        scnt[0] = (scnt[0] + 1) % len(SB)
                        sbk = SB[scnt[0]]
                        it["sbk"] = sbk
                        S = ps[sbk]

                        def sc():
                            for ci, c in enumerate(grp):
                                o = S[:, ci * 128:(ci + 1) * 128]
                                need_mask = bst["edge"] or abs(c - g) >= 2
                                ins = nc.tensor.matmul(o, lhsT=Kr.t[c % 8][p0:p0 + 64, t * 128:(t + 1) * 128],
                                                       rhs=q[p0:p0 + 64, t * 128:(t + 1) * 128], start=True, stop=not need_mask)
                                if need_mask:
                                    ins = nc.tensor.matmul(o, lhsT=Kp[p0:p0 + 16, (c % 8) * 128:(c % 8 + 1) * 128],
                                                           rhs=Qm[p0:p0 + 16, g * 128:(g + 1) * 128], start=False, stop=True)
                            return ins
                        tr.op('pe', sc, reads=[("b_k", c % 8) for c in grp] + [("b_q", qi), ("b_qm", 0), ("b_qm", 64), ("b_kp", 0),
                                                 ("b_kp", 64)],
                              writes=[psk[sbk]])

                    def finish(it):
                        bst = it["b"]
                        g, cs, h, grp, gi_ = bst["g"], bst["cs"], it["h"], it["grp"], it["gi"]
                        sbk = it["sbk"]
                        S = ps[sbk]
                        hq, hl = h // 4, h % 4
                        ob = 3 + (hq % 2)
                        O = ps[ob]
                        ai = bst["ai"]
                        ao = AO.t[ai]
                        pi_ = PT.next()
                        pt = PT.t[pi_]
                        ncol = len(grp) * 128
                        xi = XR.next()
                        xr_ = XR.t[xi]
                        tr.op('act', lambda: nc.scalar.activation(out=xr_[:, 0:ncol], in_=S[:, 0:ncol], func=AF.Exp),
                              reads=[psk[sbk]], writes=[("b_xr", xi)])
                        s0_ = grp[0] - g + 3
                        esl = Uv[:, h, s0_:s0_ + len(grp), :]
                        mcnt[0] += 1
                        if True:
                            tr.op('dve', lambda: nc.vector.tensor_tensor(
                                out=pt[:, 0:ncol].rearrange("p (s c) -> p s c", c=128),
                                in0=xr_[:, 0:ncol].rearrange("p (s c) -> p s c", c=128), in1=esl, op=ALU.mult),
                                reads=[("b_xr", xi), "b_U"], writes=[("b_pt", pi_)])

                        def pv():
                            for ci, c in enumerate(grp):
                                ins = nc.tensor.matmul(O[:, hl * 65:(hl + 1) * 65], lhsT=pt[:, ci * 128:(ci + 1) * 128],
                                                       rhs=Vr.t[c % 8][:, h * 65:(h + 1) * 65],
                                                       start=(gi_ == 0 and ci == 0), stop=(gi_ == 1 and ci == len(grp) - 1))
                            return ins
                        tr.op('pe', pv, reads=[("b_pt", pi_)] + [("b_v", c % 8) for c in grp], writes=[psk[ob]])
                        if gi_ == 1 and hl == 3:
                            ri = RC.next()
                            rc = RC.t[ri]
                            Ov = O[:, 0:260].rearrange("p (h e) -> p h e", e=65)
                            tr.op('dve', lambda: nc.vector.reciprocal(out=rc[:, :], in_=Ov[:, :, 64]),
                                  reads=[psk[ob]], writes=[("b_rc", ri)])
                            tr.op('dve', lambda: nc.vector.tensor_tensor(
                                out=ao[:, hq * 256:(hq + 1) * 256].rearrange("p (h e) -> p h e", e=64), in0=Ov[:, :, 0:64],
                                in1=rc[:, :].unsqueeze(2).to_broadcast([128, 4, 64]), op=ALU.mult),
                                reads=[psk[ob], ("b_rc", ri)], writes=[("b_ao", ai, hq)])
                        if gi_ == 1 and h == NH - 1:
                            pT = ps[5][:].bitcast(BF16)

                            def tp():
                                for t in range(8):
                                    ins = nc.tensor.transpose(pT[:, t * 128:(t + 1) * 128], ao[:, t * 128:(t + 1) * 128], identb[:])
                                return ins
                            tr.op('pe', tp, reads=[("b_ao", ai, x) for x in range(4)] + ["identb"], writes=[psk[5]])
                            ti = ATS.next()
                            tr.op('dve', lambda: nc.vector.tensor_copy(out=ATS.t[ti][:], in_=pT), reads=[psk[5]],
                                  writes=[("b_ats", ti)])
                            tr.dma(ST, AT[:, :, g * 128:(g + 1) * 128], ATS.t[ti][:].rearrange("p (k t) -> p k t", t=128),
                                   ATS.sem[ti], reads=[("b_ats", ti)], writes=[("AT", g)])
                    pend = deque()
                    for it in items:
                        start(it)
                        pend.append(it)
                        if len(pend) > LOOK:
                            finish(pend.popleft())
                    while pend:
                        finish(pend.popleft())
                phase_end()
                tr.free_sem(s0)

        def phase_p3(l):
            with contextlib.ExitStack() as st:
                Dg = sb(st, "c_dg", [128, 248 * 128], BF16)
                onesf = sb(st, "c_ones", [128, 128], F32)
                tr.op('dve', lambda: nc.vector.memset(onesf[:], 1.0 / 1024.0), writes=["c_ones"])
                for mj in range(248):
                    tr.op('dve', lambda mj=mj: nc.vector.tensor_scalar(
                        out=Dg[:, mj * 128:(mj + 1) * 128], in0=identb[:], scalar1=vcol(f"cw{l}", mj), scalar2=None, op0=ALU.mult),
                        reads=["identb", "vecs"], writes=[("c_dg", mj)])
                dgk = [("c_dg", mj) for mj in range(248)]
                UTr = Ring(tr, st, "c_ut", 2, [128, 8 * 544], BF16)
                uc = [sb(st, f"c_uc{i}", [128, 8 * 512], F32) for i in range(2)]
                sq = [sb(st, f"c_sq{i}", [128, 8 * 512], F32) for i in range(2)]
                tcount = [0]
                pend = [None]
                CTr = Ring(tr, st, "c_ct", 2, [128, 8 * 512], BF16)
                msb = [sb(st, f"c_msb{i}", [128, 512], F32) for i in range(2)]
                var = [sb(st, f"c_var{i}", [128, 512], F32) for i in range(2)]
                pi = [0]
                tl = [(base + vb0, nb) for base, n in SEGS for vb0, nb in tiles(*rng_mix(l, n), 4)]

                def load_u(slot, it):
                    g0_, nb_ = it
                    N_ = nb_ * 128
                    tr.dma('sp', UTr.t[slot][:].rearrange("p (m t) -> p m t", t=544)[:, :, 0:N_ + 32],
                           UT[:, :, g0_ * 128:g0_ * 128 + N_ + 32], UTr.sem[slot],
                           reads=[("UT", g0_ + j, ii) for j in range(-1, nb_ + 1) for ii in range(4)], writes=[("c_ut", slot)])
                ust = Stream(UTr, tl, 1, load_u)
                if True:
                    for g0, nb in tl:
                        N = nb * 128
                        t0 = g0 * 128
                        ui = ust.get()
                        ut = UTr.t[ui][:].rearrange("p (m t) -> p m t", t=544)
                        ku = ("c_ut", ui)
                        pieces = [(0, 16, g0 - 1)] + [(16 + 128 * j, 16 + 128 * (j + 1), g0 + j) for j in range(nb)] + \
                                 [(16 + N, 32 + N, g0 + nb)]
                        for a, b, gb in pieces:
                            tr.op('dve', lambda a=a, b=b, gb=gb: nc.vector.tensor_scalar(
                                out=ut[:, :, a:b], in0=ut[:, :, a:b], scalar1=vcol("valid", gb), scalar2=None, op0=ALU.mult),
                                reads=[ku, "vecs"], writes=[ku])
                        par = tcount[0] % 2
                        tcount[0] += 1
                        ucv = uc[par][:].rearrange("p (m t) -> p m t", t=512)
                        sqv = sq[par][:].rearrange("p (m t) -> p m t", t=512)
                        for m in range(8):
                            pi[0] = (pi[0] + 1) % 3
                            pb = pi[0]

                            def cv(m=m, pb=pb):
                                for j in range(31):
                                    ins = nc.tensor.matmul(ps[pb][:, 0:N], lhsT=Dg[:, (m * 31 + j) * 128:(m * 31 + j + 1) * 128],
                                                           rhs=ut[:, m, j + 1:j + 1 + N], start=(j == 0), stop=(j == 30))
                                return ins
                            tr.op('pe', cv, reads=[ku] + dgk[m * 31:(m + 1) * 31], writes=[psk[pb]])
                            tr.op('act', lambda m=m, pb=pb: nc.scalar.activation(
                                out=ucv[:, m, 0:N], in_=ps[pb][:, 0:N], func=AF.Identity, bias=vcol(f"conv_b{l}", m), scale=1.0),
                                reads=[psk[pb], "vecs"], writes=[("c_uc", par, m)])
                            tr.op('act', lambda m=m, pb=pb: nc.scalar.activation(
                                out=sqv[:, m, 0:N], in_=ps[pb][:, 0:N], func=AF.Square, bias=vcol(f"conv_b{l}", m), scale=1.0),
                                reads=[psk[pb], "vecs"], writes=[("c_sq", par, m)])

                        def stat(src, pb):
                            def f():
                                for m in range(8):
                                    ins = nc.tensor.matmul(ps[pb][:, 0:N], lhsT=onesf[:], rhs=src[:, m, 0:N], start=(m == 0), stop=(m == 7))
                                return ins
                            return f
                        pm_, pq_ = 3 + 2 * par, 4 + 2 * par
                        tr.op('pe', stat(ucv, pm_), reads=[("c_uc", par, m) for m in range(8)] + ["c_ones"], writes=[psk[pm_]])
                        tr.op('pe', stat(sqv, pq_), reads=[("c_sq", par, m) for m in range(8)] + ["c_ones"], writes=[psk[pq_]])

                        def finish(par=par, ucv=ucv, N=N, t0=t0, g0=g0, nb=nb, pm_=pm_, pq_=pq_):
                            msb_, var_ = msb[par], var[par]
                            tr.op('act', lambda: nc.scalar.copy(out=msb_[:, 0:N], in_=ps[pm_][:, 0:N]), reads=[psk[pm_]],
                                  writes=[("c_msb", par)])
                            tr.op('dve', lambda: nc.vector.tensor_tensor(out=var_[:, 0:N], in0=msb_[:, 0:N], in1=msb_[:, 0:N], op=ALU.mult),
                                  reads=[("c_msb", par)], writes=[("c_var", par)])
                            tr.op('dve', lambda: nc.vector.tensor_tensor(out=var_[:, 0:N], in0=ps[pq_][:, 0:N], in1=var_[:, 0:N],
                                                                         op=ALU.subtract),
                                  reads=[psk[pq_], ("c_var", par)], writes=[("c_var", par)])
                            tr.op('act', lambda: nc.scalar.activation(out=var_[:, 0:N], in_=var_[:, 0:N], func=AF.Sqrt, bias=EPS, scale=1.0),
                                  reads=[("c_var", par)], writes=[("c_var", par)])
                            tr.op('dve', lambda: nc.vector.reciprocal(out=var_[:, 0:N], in_=var_[:, 0:N]), reads=[("c_var", par)],
                                  writes=[("c_var", par)])
                            ci = CTr.next()
                            ct = CTr.t[ci][:].rearrange("p (m t) -> p m t", t=512)
                            for m in range(8):
                                tr.op('dve', lambda m=m: nc.vector.tensor_tensor(out=ucv[:, m, 0:N], in0=ucv[:, m, 0:N], in1=msb_[:, 0:N],
                                                                                 op=ALU.subtract),
                                      reads=[("c_uc", par, m), ("c_msb", par)], writes=[("c_uc", par, m)])
                                tr.op('dve', lambda m=m: nc.vector.tensor_tensor(out=ucv[:, m, 0:N], in0=ucv[:, m, 0:N], in1=var_[:, 0:N],
                                                                                 op=ALU.mult),
                                      reads=[("c_uc", par, m), ("c_var", par)], writes=[("c_uc", par, m)])
                                tr.op('act', lambda m=m: nc.scalar.activation(
                                    out=ct[:, m, 0:N], in_=ucv[:, m, 0:N], func=AF.Silu, bias=vcol(f"cln_b{l}", m),
                                    scale=vcol(f"cln_g{l}", m)),
                                    reads=[("c_uc", par, m), "vecs"], writes=[("c_ct", ci, m)])
                            tr.dma(ST, CT[:, :, t0:t0 + N], ct[:, :, 0:N], CTr.sem[ci], reads=[("c_ct", ci, m) for m in range(8)],
                                   writes=[("CT", g0 + j) for j in range(nb)])
                        if pend[0] is not None:
                            pend[0]()
                        pend[0] = finish
                if pend[0] is not None:
                    pend[0]()
                phase_end()

        def phase_p4(l):
            with contextlib.ExitStack() as st:
                TN = 4
                NM = TN * 128
                epi = Epi(st, ROW[f"mix_g{l}"], ROW[f"mix_b{l}"])
                W = Ring(tr, st, "d_w", 2, [128, 8192], BF16)
                ATr = Ring(tr, st, "d_at", 1, [128, 8 * NM], BF16)
                CTr = Ring(tr, st, "d_ct", 1, [128, 8 * NM], BF16)
                GTr = Ring(tr, st, "d_gt", 2, [128, 8 * NM], BF16)
                yT = sb(st, "d_yT", [128, 16 * NM], BF16)
                T1 = Ring(tr, st, "d_t1", 2, [128, 512], F32, dma=False)
                T2 = Ring(tr, st, "d_t2", 2, [128, 512], F32, dma=False)
                XP = Ring(tr, st, "d_xp", 2 * TN, [128, D], F32)
                g1B = sb(st, "d_g1B", [128, D], F32)
                gsem = tr.alloc_sem()
                Xsrc = X0 if l == 0 else X1
                xname = "X0" if l == 0 else "X1"
                pi = [0]

                def bank():
                    pi[0] = (pi[0] + 1) % 6
                    return pi[0]
                yv = yT[:].rearrange("p (k t) -> p k t", t=NM)
                tl_all = [(sidx, base + vb0, nb) for sidx, (base, n) in enumerate(SEGS) for vb0, nb in tiles(*rng_mix(l, n), TN)]

                def load_at(slot, it):
                    _, g0_, nb_ = it
                    tr.dma('sp', ATr.t[slot][:].rearrange("p (k t) -> p k t", t=NM)[:, :, 0:nb_ * 128],
                           AT[:, :, g0_ * 128:(g0_ + nb_) * 128], ATr.sem[slot], reads=[("AT", g0_ + j) for j in range(nb_)],
                           writes=[("d_at", slot)])

                def load_ct(slot, it):
                    _, g0_, nb_ = it
                    tr.dma('sp', CTr.t[slot][:].rearrange("p (k t) -> p k t", t=NM)[:, :, 0:nb_ * 128],
                           CT[:, :, g0_ * 128:(g0_ + nb_) * 128], CTr.sem[slot], reads=[("CT", g0_ + j) for j in range(nb_)],
                           writes=[("d_ct", slot)])

                def load_gt(slot, it):
                    _, g0_, nb_, j4_ = it
                    tr.dma('sp', GTr.t[slot][:].rearrange("p (k t) -> p k t", t=NM)[:, :, 0:nb_ * 128],
                           GT[:, 8 * j4_:8 * j4_ + 8, g0_ * 128:(g0_ + nb_) * 128], GTr.sem[slot],
                           reads=[("GT", g0_ + j, 8 * j4_ + o) for j in range(nb_) for o in (0, 4)], writes=[("d_gt", slot)])

                def load_xp(slot, g_):
                    tr.dma('sp', XP.t[slot][:], Xsrc[g_ * 128:(g_ + 1) * 128, :], XP.sem[slot],
                           reads=[(xname, g_)], writes=[("d_xp", slot)])
                ast = Stream(ATr, tl_all, 0, load_at)
                cst = Stream(CTr, tl_all, 0, load_ct)
                gst = Stream(GTr, [(a_, b_, c_, j4) for a_, b_, c_ in tl_all for j4 in range(4)], 1, load_gt)
                xst = Stream(XP, [g0_ + j for _, g0_, nb_ in tl_all for j in range(nb_)], 0, load_xp)
                wst = Stream(W, [(grp_, j4) for _ in tl_all for grp_ in ("ac", "out") for j4 in range(4)], 1,
                             lambda slot, it: tr.dma('sp', W.t[slot][:], ws[it[0]][l, it[1]], W.sem[slot],
                                                     reads=wkeys[(it[0], l)], writes=[("d_w", slot)]))
                cur_seg = [-1]
                pend_epi = [None]
                if True:
                    for sidx, g0, nb in tl_all:
                        if sidx != cur_seg[0]:
                            cur_seg[0] = sidx
                            tr.dma('sp', g1B[:], GBS[l * 4 + 0 + sidx], gsem, reads=[("GBS", l, 0, sidx, cg) for cg in range(4)],
                                   writes=["d_g1B"])
                        N = nb * 128
                        t0 = g0 * 128
                        ai = ast.get()
                        at = ATr.t[ai][:].rearrange("p (k t) -> p k t", t=NM)
                        ci = cst.get()
                        ct = CTr.t[ci][:].rearrange("p (k t) -> p k t", t=NM)
                        xsl = []
                        for j in range(nb):
                            xs_ = xst.get()
                            xsl.append(xs_)
                            tr.op('act', lambda xs_=xs_: nc.scalar.mul(out=XP.t[xs_][:], in_=XP.t[xs_][:], mul=float(ALPHA)),
                                  reads=[("d_xp", xs_)], writes=[("d_xp", xs_)])
                        for j4 in range(4):
                            w = wst.get()
                            wv = W.t[w][:].rearrange("p (a k c) -> p a k c", a=2, c=512)
                            kw = ("d_w", w)
                            gi = gst.get()
                            gt = GTr.t[gi][:].rearrange("p (k t) -> p k t", t=NM)
                            for m in range(4):
                                nn = 4 * j4 + m
                                pa, pc = bank(), bank()
                                for a, pb, src, ks in ((0, pa, at, ("d_at", ai)), (1, pc, ct, ("d_ct", ci))):
                                    def f(a=a, pb=pb, src=src, m=m, wv=wv):
                                        for k in range(8):
                                            ins = nc.tensor.matmul(ps[pb][:, 0:N], lhsT=wv[:, a, k, m * 128:(m + 1) * 128],
                                                                   rhs=src[:, k, 0:N], start=(k == 0), stop=(k == 7))
                                        return ins
                                    tr.op('pe', f, reads=[kw, ks], writes=[psk[pb]])
                                i1, i2 = T1.next(), T2.next()
                                tr.op('dve', lambda i1=i1, pa=pa, m=m, gt=gt: nc.vector.tensor_tensor(
                                    out=T1.t[i1][:, 0:N], in0=ps[pa][:, 0:N], in1=gt[:, m, 0:N], op=ALU.mult),
                                    reads=[psk[pa], ("d_gt", gi)], writes=[("d_t1", i1)])
                                tr.op('dve', lambda i2=i2, pc=pc, m=m, gt=gt: nc.vector.tensor_tensor(
                                    out=T2.t[i2][:, 0:N], in0=ps[pc][:, 0:N], in1=gt[:, 4 + m, 0:N], op=ALU.mult),
                                    reads=[psk[pc], ("d_gt", gi)], writes=[("d_t2", i2)])
                                tr.op('dve', lambda i1=i1, i2=i2, nn=nn: nc.vector.tensor_tensor(
                                    out=yv[:, nn, 0:N], in0=T1.t[i1][:, 0:N], in1=T2.t[i2][:, 0:N], op=ALU.add),
                                    reads=[("d_t1", i1), ("d_t2", i2)], writes=[("d_yT", nn)])
                        ast.prefetch()
                        cst.prefetch()
                        if pend_epi[0] is not None:
                            pend_epi[0]()
                            pend_epi[0] = None
                        for c in range(4):
                            w = wst.get()
                            wv = W.t[w][:].rearrange("p (k c) -> p k c", c=512)
                            kw = ("d_w", w)
                            for j in range(nb):
                                pb = bank()

                                def f(pb=pb, j=j, wv=wv):
                                    for k in range(16):
                                        ins = nc.tensor.matmul(ps[pb][:, :], lhsT=yv[:, k, j * 128:(j + 1) * 128], rhs=wv[:, k, :],
                                                               start=(k == 0), stop=(k == 15))
                                    return ins
                                tr.op('pe', f, reads=[kw] + [("d_yT", nn) for nn in range(16)], writes=[psk[pb]])
                                i1 = T1.next()
                                tr.op('dve', lambda i1=i1, pb=pb, c=c: nc.vector.tensor_tensor(
                                    out=T1.t[i1][:, :], in0=ps[pb][:, :], in1=g1B[:, c * 512:(c + 1) * 512], op=ALU.mult),
                                    reads=[psk[pb], "d_g1B"], writes=[("d_t1", i1)])
                                xp = XP.t[xsl[j]]
                                tr.op('dve', lambda i1=i1, xp=xp, c=c: nc.vector.tensor_tensor(
                                    out=xp[:, c * 512:(c + 1) * 512], in0=xp[:, c * 512:(c + 1) * 512],
                                    in1=T1.t[i1][:, :], op=ALU.add),
                                    reads=[("d_t1", i1), ("d_xp", xsl[j])], writes=[("d_xp", xsl[j])])
                        def do_epi(g0=g0, nb=nb, xsl=xsl, sidx=sidx):
                            for j in range(nb):
                                g = g0 + j
                                epi.run(XP.t[xsl[j]][:], ("d_xp", xsl[j]), XP.sem[xsl[j]], g, XM[g * 128:(g + 1) * 128, :],
                                        H2T[:, :, 1 + g * 128:1 + (g + 1) * 128],
                                        S=S_of(l, 1, sidx), T=T_of(l, 1, sidx),
                                        xkey=("XM", g), hkey=("H2T", g))
                        pend_epi[0] = do_epi
                if pend_epi[0] is not None:
                    pend_epi[0]()
                phase_end()
                tr.free_sem(gsem)

        def phase_p5(l):
            with contextlib.ExitStack() as st:
                last = (l == DEPTH - 1)
                epi = Epi(st, ROW[f"ffn_g{l}"], ROW[f"ffn_b{l}"])
                W = Ring(tr, st, "f_w", 3, [128, 8192], BF16)
                H2 = sb(st, "f_h2", [128, 16 * 514], BF16)
                hsem = tr.alloc_sem()
                ASB = Ring(tr, st, "f_a", 2, [128, 514], BF16, dma=False)
                DG = Ring(tr, st, "f_dg", 2, [128, 3 * 128], BF16, dma=False)
                GEL = Ring(tr, st, "f_gel", 2, [128, 512], F32, dma=False)
                gT = sb(st, "f_gT", [128, NFC * 512], BF16)
                XP = Ring(tr, st, "f_xp", 4, [128, D], F32)
                T1 = Ring(tr, st, "f_t1", 2, [128, 512], F32, dma=False)
                g2B = sb(st, "f_g2B", [128, D], F32)
                gsem = tr.alloc_sem()
                h2 = H2[:].rearrange("p (k t) -> p k t", t=514)
                gv = gT[:].rearrange("p (k t) -> p k t", t=512)
                pi = [0]

                def bank():
                    pi[0] = (pi[0] + 1) % 8
                    return pi[0]
                tl_all = [(sidx, base, base + vb0, nb) for sidx, (base, n) in enumerate(SEGS) for vb0, nb in tiles(*rng_ffn(l, n), 4)]

                def load_w(slot, it):
                    grp_, idx_ = it
                    if grp_ == "up":
                        tr.dma('sp', W.t[slot][:], ws["up"][l, idx_], W.sem[slot], reads=wkeys[("up", l)], writes=[("f_w", slot)])
                    else:
                        tr.dma('sp', W.t[slot][:, 0:5632], ws["down"][l, idx_], W.sem[slot], reads=wkeys[("down", l)],
                               writes=[("f_w", slot)])
                wst = Stream(W, [it for _ in tl_all for it in ([("up", i) for i in range(22)] + [("down", i) for i in range(16)])],
                             2, load_w)

                def load_h2(ti):
                    _, _, g0_, nb_ = tl_all[ti]
                    tr.dma('sp', h2[:, :, 0:nb_ * 128 + 2], H2T[:, :, g0_ * 128:(g0_ + nb_) * 128 + 2], hsem,
                           reads=[("H2T", g0_ + j) for j in range(-1, nb_ + 1)], writes=["f_h2"])
                load_h2(0)
                pend_epi = [None]
                cur_seg = [-1]
                if True:
                    for ti, (sidx, base, g0, nb) in enumerate(tl_all):
                        if sidx != cur_seg[0]:
                            cur_seg[0] = sidx
                            tr.dma('sp', g2B[:], GBS[l * 4 + 2 + sidx], gsem, reads=[("GBS", l, 1, sidx, cg) for cg in range(4)],
                                   writes=["f_g2B"])
                        N = nb * 128
                        t0 = g0 * 128
                        kh = "f_h2"
                        for i in range(22):
                            if i == 6 and pend_epi[0] is not None:
                                pend_epi[0]()
                                pend_epi[0] = None
                            w = wst.get()
                            wv = W.t[w][:].rearrange("p (k c) -> p k c", c=512)
                            kw = ("f_w", w)
                            for r in range(2):
                                m = 2 * i + r
                                if m >= NFC:
                                    continue
                                pa, pbb, pcv, pe_ = bank(), bank(), bank(), bank()

                                def fa(pa=pa, pe_=pe_, r=r, wv=wv):
                                    for k in range(16):
                                        nc.tensor.matmul(ps[pa][:, 0:N], lhsT=wv[:, k, r * 128:(r + 1) * 128], rhs=h2[:, k, 1:N + 1],
                                                         start=(k == 0), stop=(k == 15))
                                    for k in range(16):
                                        ins = nc.tensor.matmul(ps[pe_][:, 0:2], lhsT=wv[:, k, r * 128:(r + 1) * 128],
                                                               rhs=h2[:, k, 0:N + 2:N + 1], start=(k == 0), stop=(k == 15))
                                    return ins
                                tr.op('pe', fa, reads=[kw, kh], writes=[psk[pa], psk[pe_]])

                                def fb(pbb=pbb, r=r, wv=wv):
                                    for k in range(16):
                                        ins = nc.tensor.matmul(ps[pbb][:, 0:N], lhsT=wv[:, k, (2 + r) * 128:(3 + r) * 128],
                                                               rhs=h2[:, k, 1:N + 1], start=(k == 0), stop=(k == 15))
                                    return ins
                                tr.op('pe', fb, reads=[kw, kh], writes=[psk[pbb]])
                                ai = ASB.next()
                                asb = ASB.t[ai]
                                tr.op('act', lambda asb=asb, pa=pa: nc.scalar.copy(out=asb[:, 1:N + 1], in_=ps[pa][:, 0:N]),
                                      reads=[psk[pa]], writes=[("f_a", ai, 0)])
                                tr.op('act', lambda asb=asb, pe_=pe_: nc.scalar.copy(out=asb[:, 0:N + 2:N + 1], in_=ps[pe_][:, 0:2]),
                                      reads=[psk[pe_]], writes=[("f_a", ai, 1)])
                                di = DG.next()
                                dg = DG.t[di]
                                for jj in range(3):
                                    tr.op('dve', lambda jj=jj, dg=dg, m=m: nc.vector.tensor_scalar(
                                        out=dg[:, jj * 128:(jj + 1) * 128], in0=identb[:], scalar1=vcol(f"fcw{l}", jj * NFC + m),
                                        scalar2=None, op0=ALU.mult), reads=["identb", "vecs"], writes=[("f_dg", di, jj)])

                                def fc(pcv=pcv, dg=dg, asb=asb):
                                    for jj in range(3):
                                        ins = nc.tensor.matmul(ps[pcv][:, 0:N], lhsT=dg[:, jj * 128:(jj + 1) * 128], rhs=asb[:, jj:jj + N],
                                                               start=(jj == 0), stop=(jj == 2))
                                    return ins
                                tr.op('pe', fc, reads=[("f_a", ai, 0), ("f_a", ai, 1)] + [("f_dg", di, jj) for jj in range(3)],
                                      writes=[psk[pcv]])
                                gi = GEL.next()
                                gel = GEL.t[gi]
                                tr.op('act', lambda gel=gel, pcv=pcv, m=m: nc.scalar.activation(
                                    out=gel[:, 0:N], in_=ps[pcv][:, 0:N], func=AF.Gelu, bias=vcol(f"fcb{l}", m), scale=1.0),
                                    reads=[psk[pcv], "vecs"], writes=[("f_gel", gi)])
                                tr.op('dve', lambda gel=gel, pbb=pbb, m=m: nc.vector.tensor_tensor(
                                    out=gv[:, m, 0:N], in0=ps[pbb][:, 0:N], in1=gel[:, 0:N], op=ALU.mult),
                                    reads=[psk[pbb], ("f_gel", gi)], writes=[("f_gT", m)])
                        if ti + 1 < len(tl_all):
                            load_h2(ti + 1)
                        if pend_epi[0] is not None:
                            pend_epi[0]()
                            pend_epi[0] = None
                        for j in range(nb):
                            tr.dma('sp', XP.t[j][:], XM[(g0 + j) * 128:(g0 + j + 1) * 128, :], XP.sem[j],
                                   reads=[("XM", g0 + j)], writes=[("f_xp", j)])
                            tr.op('act', lambda j=j: nc.scalar.mul(out=XP.t[j][:], in_=XP.t[j][:], mul=float(ALPHA)),
                                  reads=[("f_xp", j)], writes=[("f_xp", j)])
                        kranges = [(0, 11), (11, 22), (22, 33), (33, 43)]
                        for c in range(4):
                            banks = [4 * (c % 2) + j for j in range(nb)]
                            for kg, (k0, k1) in enumerate(kranges):
                                w = wst.get()
                                wv = W.t[w][:, 0:5632].rearrange("p (k c) -> p k c", c=512)
                                kw = ("f_w", w)
                                for j in range(nb):
                                    def fd(j=j, wv=wv, k0=k0, k1=k1, kg=kg):
                                        for k in range(k0, k1):
                                            ins = nc.tensor.matmul(ps[banks[j]][:, :], lhsT=gv[:, k, j * 128:(j + 1) * 128],
                                                                   rhs=wv[:, k - k0, :], start=(k == 0), stop=(k == NFC - 1))
                                        return ins
                                    tr.op('pe', fd, reads=[kw] + [("f_gT", k) for k in range(k0, k1)], writes=[psk[banks[j]]])
                            for j in range(nb):
                                i1 = T1.next()
                                tr.op('dve', lambda i1=i1, j=j, c=c: nc.vector.tensor_tensor(
                                    out=T1.t[i1][:, :], in0=ps[banks[j]][:, :], in1=g2B[:, c * 512:(c + 1) * 512], op=ALU.mult),
                                    reads=[psk[banks[j]], "f_g2B"], writes=[("f_t1", i1)])
                                xp = XP.t[j]
                                tr.op('dve', lambda i1=i1, xp=xp, c=c: nc.vector.tensor_tensor(
                                    out=xp[:, c * 512:(c + 1) * 512], in0=xp[:, c * 512:(c + 1) * 512],
                                    in1=T1.t[i1][:, :], op=ALU.add),
                                    reads=[("f_t1", i1), ("f_xp", j)], writes=[("f_xp", j)])
                        def do_epi(g0=g0, nb=nb, sidx=sidx, base=base):
                            for j in range(nb):
                                g = g0 + j
                                if last:
                                    ob = (g - base - HALO) + (0 if sidx == 0 else 32)
                                    epi.run(XP.t[j][:], ("f_xp", j), XP.sem[j], g, yout[ob * 128:(ob + 1) * 128, :], None)
                                else:
                                    epi.run(XP.t[j][:], ("f_xp", j), XP.sem[j], g, X1[g * 128:(g + 1) * 128, :],
                                            H1T[:, :, 1 + g * 128:1 + (g + 1) * 128], S=S_of(l + 1, 0, sidx),
                                            T=T_of(l + 1, 0, sidx), xkey=("X1", g), hkey=("H1T", g))
                        pend_epi[0] = do_epi
                if pend_epi[0] is not None:
                    pend_epi[0]()
                phase_end()
                tr.free_sem(hsem)
                tr.free_sem(gsem)

        phases = [("pre", phase_modpre)]
        for l in range(DEPTH):
            phases += [(f"p1_{l}", lambda l=l: phase_p1(l)), (f"p2_{l}", lambda l=l: phase_p2(l)),
                       (f"p3_{l}", lambda l=l: phase_p3(l)), (f"p4_{l}", lambda l=l: phase_p4(l)),
                       (f"p5_{l}", lambda l=l: phase_p5(l))]
        for name, fn in phases:
            fn()
            if stop_after == name:
                break
        tr.barrier()
        print(f"[build] instructions={tr.nins} waits={tr.nwait}")
    return nc


_CACHE = {}


def make_in_maps(inp):
    inp = {k: np.asarray(v, dtype=np.float32) for k, v in inp.items()}
    wts = host_weights(inp)
    shared_cols, rows, U, kp, ident, sel = host_shared_tables(inp)
    in_maps = []
    for core in range(8):
        xin, vecs, qm = host_core_tables(inp, core, shared_cols)
        m = {"xin": xin, "vecs": vecs, "rows": rows, "utab": U, "qmask": qm, "kpat": kp, "ident": ident, "sel": sel}
        m.update(wts)
        in_maps.append(m)
    return in_maps


def kernel(**inputs):
    in_maps = make_in_maps(inputs)
    if "nc" not in _CACHE:
        _CACHE["nc"] = build()
    res = run_bass_kernel_spmd(_CACHE["nc"], in_maps, core_ids=list(range(8)))
    yp = np.zeros((2, 16384, D), np.float32)
    ys = np.zeros((4, 2048, D), np.float32)
    for core in range(8):
        y = res.results[core]["yout"]
        pb, pj = core // 4, core % 4
        sbi, sj = core // 2, core % 2
        yp[pb, pj * 4096:(pj + 1) * 4096] = y[0:4096]
        ys[sbi, sj * 1024:(sj + 1) * 1024] = y[4096:5120]
    return (yp, ys)
```

```python
import contextlib
import numpy as np
import concourse.bass as bass
import concourse.mybir as mybir
from concourse.bass_utils import run_bass_kernel_spmd

F32 = mybir.dt.float32
BF16 = mybir.dt.bfloat16
AF = mybir.ActivationFunctionType
ALU = mybir.AluOpType

D = 2048
DEPTH = 2
NH = 16
DFF = 5504
NFC = 43
ALPHA = (2 * DEPTH) ** 0.25
EPS = 1e-5
NEG = -30000.0
HALO = 6
SEGS = [(0, 32), (44, 8)]
NBT = 64
NTOK = NBT * 128
SAME_SYNC = True


def rng_in(l, n):
    return [(0, n + 12), (3, n + 9)][l]


def rng_mix(l, n):
    return [(2, n + 10), (5, n + 7)][l]


def rng_ffn(l, n):
    return [(3, n + 9), (6, n + 6)][l]


def tiles(lo, hi, step):
    out = []
    b = lo
    while b < hi:
        nb = min(step, hi - b)
        out.append((b, nb))
        b += nb
    return out


def tiles_bal(lo, hi, step):
    t = tiles(lo, hi, step)
    if len(t) >= 2 and t[-1][1] < step - 1:
        tot = t[-2][1] + t[-1][1]
        a = (tot + 1) // 2
        t[-2] = (t[-2][0], a)
        t[-1] = (t[-2][0] + a, tot - a)
    return t


class Tr:
    def __init__(self, nc, es, nsem=100):
        self.nc = nc
        self.eng = {'pe': nc.tensor, 'act': nc.scalar, 'dve': nc.vector, 'pool': nc.gpsimd, 'sp': nc.sync}
        self.sems = [es.enter_context(nc.semaphore(f"s{i}")) for i in range(nsem)]
        self.cnt = [0] * nsem
        self.esem = {e: i for i, e in enumerate(['pe', 'act', 'dve', 'pool'])}
        self.free = list(range(4, nsem))
        self.waited = {e: {} for e in self.eng}
        self.lastw = {}
        self.rd = {}
        self.nins = 0
        self.nwait = 0

    def alloc_sem(self):
        return self.free.pop()

    def free_sem(self, s):
        self.free.append(s)

    def _deps(self, reads, writes):
        d = {}
        for k in reads:
            t = self.lastw.get(k)
            if t is not None and d.get(t[0], 0) < t[1]:
                d[t[0]] = t[1]
        for k in writes:
            t = self.lastw.get(k)
            if t is not None and d.get(t[0], 0) < t[1]:
                d[t[0]] = t[1]
            r = self.rd.get(k)
            if r:
                for s, v in r.items():
                    if d.get(s, 0) < v:
                        d[s] = v
        return d

    def _wait(self, e, d):
        w = self.waited[e]
        own = self.esem.get(e)
        for s, v in d.items():
            if s >= 4:
                v = max(v, self.cnt[s])
            if s == own and (e == 'pe' or not SAME_SYNC):
                continue
            if w.get(s, 0) < v:
                self.eng[e].wait_ge(self.sems[s], v)
                w[s] = v
                self.nwait += 1

    def _record(self, tok, reads, writes):
        for k in writes:
            self.lastw[k] = tok
            self.rd[k] = {}
        for k in reads:
            r = self.rd.setdefault(k, {})
            if r.get(tok[0], 0) < tok[1]:
                r[tok[0]] = tok[1]

    def op(self, e, fn, reads=(), writes=()):
        self._wait(e, self._deps(reads, writes))
        ins = fn()
        s = self.esem[e]
        self.cnt[s] += 1
        ins.then_inc(self.sems[s], 1)
        self.nins += 1
        self._record((s, self.cnt[s]), reads, writes)

    def dma(self, q, out, in_, sem, reads=(), writes=()):
        self._wait(q, self._deps(reads, writes))
        ins = self.eng[q].dma_start(out=out, in_=in_)
        self.cnt[sem] += 16
        ins.then_inc(self.sems[sem], 16)
        self.nins += 1
        self._record((sem, self.cnt[sem]), reads, writes)

    def barrier(self, exclude=()):
        d = {s: c for s, c in enumerate(self.cnt) if c > 0 and s not in exclude}
        for e in self.eng:
            w = self.waited[e]
            for s, v in d.items():
                if w.get(s, 0) < v:
                    self.eng[e].wait_ge(self.sems[s], v)
                    w[s] = v
        self.lastw = {k: t for k, t in self.lastw.items() if t[0] in exclude}
        self.rd = {}


class Ring:
    uid = 0

    def __init__(self, tr, es, name, n, shape, dtype, dma=True):
        self.tr = tr
        self.name = name
        self.n = n
        Ring.uid += 1
        self.t = [es.enter_context(tr.nc.sbuf_tensor(f"{name}_{i}_u{Ring.uid}", shape, dtype)) for i in range(n)]
        self.sem = [tr.alloc_sem() if dma else None for _ in range(n)]
        self.i = -1
        es.callback(self._release)

    def _release(self):
        for s in self.sem:
            if s is not None:
                self.tr.free_sem(s)

    def next(self):
        self.i = (self.i + 1) % self.n
        return self.i


class Stream:
    def __init__(self, ring, items, depth, load_fn):
        self.ring, self.items, self.depth, self.load_fn = ring, list(items), depth, load_fn
        self.issued = 0
        self.taken = 0

    def get(self):
        while self.issued < min(len(self.items), self.taken + 1 + self.depth):
            self.load_fn(self.issued % self.ring.n, self.items[self.issued])
            self.issued += 1
        slot = self.taken % self.ring.n
        self.taken += 1
        return slot

    def prefetch(self):
        if self.issued < len(self.items) and self.issued <= self.taken:
            self.load_fn(self.issued % self.ring.n, self.items[self.issued])
            self.issued += 1


ST = 'act'


class Cols:
    def __init__(self):
        self.parts = []
        self.off = {}
        self.n = 0

    def add(self, name, arr):
        arr = np.ascontiguousarray(arr, dtype=np.float32).reshape(128, -1)
        self.off[name] = self.n
        self.n += arr.shape[1]
        self.parts.append(arr)

    def build(self):
        return np.ascontiguousarray(np.concatenate(self.parts, axis=1))


def col_layout():
    off = {}
    n = 0

    def add(name, w):
        nonlocal n
        off[name] = n
        n += w
    for l in range(DEPTH):
        add(f"b_in{l}", 72)
        add(f"conv_b{l}", 8)
        add(f"cln_g{l}", 8)
        add(f"cln_b{l}", 8)
        add(f"fcb{l}", NFC)
        add(f"fcw{l}", 3 * NFC)
        add(f"cw{l}", 8 * 31)
        add(f"bmod{l}", 64)
    add("cT", 32)
    add("valid", NBT)
    return off, n


COFF, NCOLS = col_layout()
ROW = {"ln_in_g": 0, "ln_in_b": 1}
for _l in range(DEPTH):
    for _i, _nm in enumerate(["mix_g", "mix_b", "ffn_g", "ffn_b", "bv", "bg1", "bg2"]):
        ROW[f"{_nm}{_l}"] = 2 + 7 * _l + _i
NROWS = 2 + 7 * DEPTH

def w_in_perm():
    q = np.arange(0, 1024)
    k = np.arange(1024, 2048)
    v = np.arange(2048, 3072)
    uval = np.arange(3072, 4096)
    ugate = np.arange(4096, 5120)
    gates = np.arange(5120, 9216)
    cols = [q, k, v]
    u = []
    for i in range(4):
        u.append(uval[256 * i:256 * (i + 1)])
        u.append(ugate[256 * i:256 * (i + 1)])
    cols += u
    cols.append(gates)
    return np.concatenate(cols)


W_IN_PERM = w_in_perm()


def tile_w(w, kc, cols_per_tile):
    K, N = w.shape
    assert K == kc * 128 and N % cols_per_tile == 0
    nt = N // cols_per_tile
    a = w.reshape(kc, 128, nt, cols_per_tile).transpose(2, 1, 0, 3)
    return np.ascontiguousarray(a).reshape(nt, 128, kc * cols_per_tile)


def host_weights(inp):
    out = {}
    wm, wi, wac, wo, wu, wd = [], [], [], [], [], []
    for l in range(DEPTH):
        w_mod = inp['w_mod'][l]
        perm = np.concatenate([np.arange(0, 2048), np.arange(2048, 4096), np.arange(6144, 8192),
                               np.arange(8192, 10240), np.arange(4096, 6144), np.arange(10240, 12288)])
        wm.append(tile_w(w_mod[:, perm], 16, 512))
        wi.append(tile_w(inp['w_in'][l][:, W_IN_PERM], 16, 512))
        wa = tile_w(inp['w_attn_proj'][l], 8, 512)
        wc = tile_w(inp['w_conv_proj'][l], 8, 512)
        wac.append(np.concatenate([wa, wc], axis=2))
        wo.append(tile_w(inp['w_out'][l], 16, 512))
        w_up = inp['w_up'][l]
        a = w_up[:, :DFF]
        b = w_up[:, DFF:]
        pad = np.zeros((D, 128), np.float32)
        cols = []
        for i in range(22):
            for src in (a, b):
                for r in range(2):
                    m = 2 * i + r
                    cols.append(src[:, 128 * m:128 * (m + 1)] if m < NFC else pad)
        wu.append(tile_w(np.concatenate(cols, axis=1), 16, 512))
        w_dn = inp['w_down'][l]
        w_dn = np.concatenate([w_dn, np.zeros((128, D), np.float32)], axis=0)
        t = []
        for c in range(4):
            for kg in range(4):
                blk = w_dn[kg * 11 * 128:(kg + 1) * 11 * 128, c * 512:(c + 1) * 512]
                t.append(tile_w(blk, 11, 512)[0])
        wd.append(np.stack(t))
    out['wf_mod'] = np.stack(wm)
    out['wf_in'] = np.stack(wi)
    out['wf_ac'] = np.stack(wac)
    out['wf_out'] = np.stack(wo)
    out['wf_up'] = np.stack(wu)
    out['wf_down'] = np.stack(wd)
    return out


WGROUPS = [("mod", 24, 8192), ("in", 18, 8192), ("ac", 4, 8192), ("out", 4, 8192), ("up", 22, 8192),
           ("down", 16, 5632)]


def chunkmajor(v):
    return np.ascontiguousarray(v.reshape(-1, 128).T)


def host_shared_tables(inp):
    c = Cols()
    for l in range(DEPTH):
        c.add(f"b_in{l}", chunkmajor(inp['b_in'][l][W_IN_PERM]))
        c.add(f"conv_b{l}", chunkmajor(inp['conv_b'][l]))
        c.add(f"cln_g{l}", chunkmajor(inp['conv_ln_g'][l]))
        c.add(f"cln_b{l}", chunkmajor(inp['conv_ln_b'][l]))
        c.add(f"fcb{l}", chunkmajor(inp['ffn_conv_b'][l]))
        fw = inp['ffn_conv_w'][l]
        c.add(f"fcw{l}", np.concatenate([chunkmajor(fw[j]) for j in range(3)], axis=1))
        cw = inp['conv_w'][l]
        c.add(f"cw{l}", cw.T.reshape(8, 128, 31).transpose(1, 0, 2).reshape(128, 248))
        bm = inp['b_mod'][l]
        c.add(f"bmod{l}", chunkmajor(np.concatenate([bm[0:2048], bm[2048:4096], bm[6144:8192], bm[8192:10240]])))
    shared = c
    rows = np.zeros((NROWS, D), np.float32)
    rows[ROW["ln_in_g"]] = inp['ln_in_g']
    rows[ROW["ln_in_b"]] = inp['ln_in_b']
    for l in range(DEPTH):
        rows[ROW[f"mix_g{l}"]] = inp['ln_mix_g'][l]
        rows[ROW[f"mix_b{l}"]] = inp['ln_mix_b'][l]
        rows[ROW[f"ffn_g{l}"]] = inp['ln_ffn_g'][l]
        rows[ROW[f"ffn_b{l}"]] = inp['ln_ffn_b'][l]
        rows[ROW[f"bv{l}"], :1024] = inp['b_in'][l][2048:3072]
        rows[ROW[f"bg1{l}"]] = inp['b_mod'][l][4096:6144]
        rows[ROW[f"bg2{l}"]] = inp['b_mod'][l][10240:12288]
    rpb = inp['na_rpb']
    kc = np.arange(64)[:, None]
    qc = np.arange(64)[None, :]
    cs = np.clip(qc - 8, 0, 48)
    colmask = (kc >= cs) & (kc < cs + 16)
    dci = np.clip(kc - qc + 15, 0, 30)
    U = np.full((DEPTH, 2, 64, NH, 7, 2, 64), NEG, np.float32)
    for a in range(2):
        for sdx in range(7):
            for b in range(2):
                dr = 2 * (sdx - 3) + a - b
                if -7 <= dr <= 7:
                    blk = rpb[:, :, dr + 7, :][:, :, dci]
                    blk = np.where(colmask[None, None], blk, np.float32(NEG))
                    U[:, a, :, :, sdx, b, :] = blk.transpose(0, 2, 1, 3)
    U = U.reshape(DEPTH, 128, NH * 896)
    kp = np.zeros((16, 8, 128), np.float32)
    for b in range(8):
        for a in range(2):
            kp[(2 * b + a) % 16, b, a * 64:(a + 1) * 64] = 1.0
    ident = np.eye(128, dtype=np.float32)
    sel = np.zeros((3, 2, 128), np.float32)
    sel[0, 0] = 1.0
    sel[1, 1] = 1.0
    sel[2, :] = 1.0
    return shared, rows, np.ascontiguousarray(U), kp.reshape(16, 1024), ident, sel.reshape(3, 256)


def core_geometry(core):
    pj, sj = core % 4, core % 2
    real = np.full(NBT, -1, np.int64)
    nblk = np.zeros(NBT, np.int64)
    for vb in range(44):
        rb = 32 * pj - HALO + vb
        nblk[vb] = 128
        if 0 <= rb < 128:
            real[vb] = rb
    for vb in range(20):
        rb = 8 * sj - HALO + vb
        nblk[44 + vb] = 16
        if 0 <= rb < 16:
            real[44 + vb] = rb
    return real, nblk


def host_core_tables(inp, core, shared_cols):
    pb, sb = core // 4, core // 2
    real, nblk = core_geometry(core)
    xin = np.zeros((NTOK, D), np.float32)
    for g in range(NBT):
        if real[g] >= 0:
            src = inp['x_prompt'][pb] if g < 44 else inp['x_sample'][sb]
            xin[g * 128:(g + 1) * 128] = src[real[g] * 128:(real[g] + 1) * 128]
    c = Cols()
    c.parts = list(shared_cols.parts)
    c.off = dict(shared_cols.off)
    c.n = shared_cols.n
    cpair = np.stack([inp['c_prompt'][pb], inp['c_sample'][sb]], axis=1)
    c.add("cT", cpair.reshape(16, 128, 2).transpose(1, 0, 2).reshape(128, 32))
    valid = (real >= 0).astype(np.float32)
    c.add("valid", np.broadcast_to(valid[None, :], (128, NBT)))
    assert c.off == COFF and c.n == NCOLS
    qm = np.zeros((16, NTOK), np.float32)
    for g in range(NBT):
        base = 0 if g < 44 else 44
        nseg = 44 if g < 44 else 20
        for a in range(2):
            cols = slice(g * 128 + a * 64, g * 128 + (a + 1) * 64)
            if real[g] < 0:
                continue
            rows_total = nblk[g] * 2
            r = real[g] * 2 + a
            r0 = min(max(r - 4, 0), rows_total - 8)
            qv = 2 * g + a
            for j in range(16):
                ok = False
                for kv in range(qv - 7, qv + 9):
                    if kv % 16 != j:
                        continue
                    kg = kv // 2
                    if kg < base or kg >= base + nseg or real[kg] < 0:
                        continue
                    kr = real[kg] * 2 + (kv % 2)
                    if r0 <= kr < r0 + 8:
                        ok = True
                qm[j, cols] = 0.0 if ok else NEG
    return xin, c.build(), qm


def build(stop_after=None, debug_outs=()):
    nc = bass.Bass("TRN2", target_bir_lowering=False)

    def dram(name, shape, dt, kind="Internal"):
        if name in debug_outs:
            kind = "ExternalOutput"
        return nc.dram_tensor(name, shape, dt, kind=kind).ap()

    xin = dram("xin", [NTOK, D], F32, "ExternalInput")
    vecs_d = dram("vecs", [128, NCOLS], F32, "ExternalInput")
    rows_d = dram("rows", [NROWS, D], F32, "ExternalInput")
    utab_d = dram("utab", [DEPTH, 128, NH * 896], F32, "ExternalInput")
    qmask_d = dram("qmask", [16, NTOK], F32, "ExternalInput")
    kpat_d = dram("kpat", [16, 1024], F32, "ExternalInput")
    ident_d = dram("ident", [128, 128], F32, "ExternalInput")
    sel_d = dram("sel", [3, 256], F32, "ExternalInput")
    wf = {g: dram(f"wf_{g}", [DEPTH, nt, 128, w], F32, "ExternalInput") for g, nt, w in WGROUPS}
    ws = {g: dram(f"ws_{g}", [DEPTH, nt, 128, w], BF16) for g, nt, w in WGROUPS}
    yout = dram("yout", [40 * 128, D], F32, "ExternalOutput")

    X0 = dram("X0", [NTOK, D], F32)
    XM = dram("XM", [NTOK, D], F32)
    X1 = dram("X1", [NTOK, D], F32)
    H1T = dram("H1T", [128, 16, NTOK + 2], BF16)
    H2T = dram("H2T", [128, 16, NTOK + 2], BF16)
    QT = dram("QT", [NBT, 128, 1024], BF16)
    KT = dram("KT", [NBT, 128, 1024], BF16)
    VV = dram("VV", [NBT, 128, 1040], BF16)
    UT = dram("UT", [128, 8, NTOK + 32], BF16)
    GT = dram("GT", [128, 32, NTOK], BF16)
    AT = dram("AT", [128, 8, NTOK], BF16)
    CT = dram("CT", [128, 8, NTOK], BF16)
    GBS = dram("GBS", [DEPTH * 4, 128, D], F32)

    es = contextlib.ExitStack()
    with es:
        tr = Tr(nc, es)
        ps = [es.enter_context(nc.psum_tensor(f"ps{i}", [128, 512], F32)) for i in range(8)]
        psk = [("ps", i) for i in range(8)]

        def sb(stack, name, shape, dt):
            Ring.uid += 1
            return stack.enter_context(nc.sbuf_tensor(f"{name}_u{Ring.uid}", shape, dt))

        vecs = sb(es, "vecs_sb", [128, NCOLS], F32)
        identb = sb(es, "identb", [128, 128], BF16)
        onesrow = sb(es, "onesrow", [1, 128], BF16)
        modT = sb(es, "modT", [128, DEPTH, 64, 2], F32)
        csem = tr.alloc_sem()
        tr.dma('sp', vecs[:], vecs_d[:, :], csem, writes=["vecs"])
        tr.dma('pool', identb[:], ident_d[:, :], csem, writes=["identb"])
        tr.op('dve', lambda: nc.vector.memset(onesrow[:], 1.0), writes=["onesrow"])

        def vcol(name, i=0, w=1):
            o = COFF[name] + i
            return vecs[:, o:o + w]

        cast_sems = set()
        wkeys = {}
        order = [(0, "mod"), (1, "mod")]
        for l in range(DEPTH):
            order += [(l, g) for g in ["in", "ac", "out", "up", "down"]]
        gdims = {g: (nt, w) for g, nt, w in WGROUPS}
        for l, g in order:
            nt, w = gdims[g]
            s = tr.alloc_sem()
            cast_sems.add(s)
            keys = []
            for t0 in range(0, nt, 2):
                t1 = min(nt, t0 + 2)
                k = ("ws", g, l, t0)
                keys.append(k)
                tr.dma('pool', ws[g][l, t0:t1].rearrange("t p c -> p t c"),
                       wf[g][l, t0:t1].rearrange("t p c -> p t c"), s, writes=[k])
            wkeys[(g, l)] = keys

        def phase_end():
            tr.barrier(exclude=cast_sems)

        class Epi:
            def __init__(self, st, rowg, rowb):
                self.xb = Ring(tr, st, "e_xb", 2, [128, D], BF16, dma=False)
                self.hts = Ring(tr, st, "e_hts", 3, [128, D], BF16)
                self.stt = sb(st, "e_st", [128, 4, 6], F32)
                self.mv = sb(st, "e_mv", [128, 8], F32)
                self.tv = Ring(tr, st, "e_tv", 2, [128, 32], F32, dma=False)
                self.gB = sb(st, "e_gB", [128, D], F32)
                self.bB = sb(st, "e_bB", [128, D], F32)
                self.sem = tr.alloc_sem()
                st.callback(lambda: tr.free_sem(self.sem))
                tr.dma('sp', self.gB[:], rows_d[rowg:rowg + 1, :].partition_broadcast(128), self.sem, writes=["e_gB"])
                tr.dma('sp', self.bB[:], rows_d[rowb:rowb + 1, :].partition_broadcast(128), self.sem, writes=["e_bB"])

            def run(self, xp, kxp, xsem, g, xdst, hdst, S=None, T=None, xkey=None, hkey=None):
                stt, mv = self.stt, self.mv
                for i in range(4):
                    tr.op('dve', lambda i=i: nc.vector.bn_stats(out=stt[:, i, :], in_=xp[:, i * 512:(i + 1) * 512]),
                          reads=[kxp], writes=[("e_st", i)])
                tr.op('dve', lambda: nc.vector.bn_aggr(out=mv[:, 0:2], in_=stt[:]),
                      reads=[("e_st", i) for i in range(4)], writes=["e_mv"])
                tr.op('act', lambda: nc.scalar.activation(out=mv[:, 2:3], in_=mv[:, 1:2], func=AF.Sqrt, bias=EPS, scale=1.0),
                      reads=["e_mv"], writes=["e_sd"])
                tr.op('dve', lambda: nc.vector.reciprocal(out=mv[:, 3:4], in_=mv[:, 2:3]), reads=["e_sd"], writes=["e_rstd"])
                tr.op('dve', lambda: nc.vector.scalar_tensor_tensor(out=mv[:, 4:5], in0=mv[:, 0:1], scalar=-1.0, in1=mv[:, 3:4],
                                                                    op0=ALU.mult, op1=ALU.mult),
                      reads=["e_mv", "e_rstd"], writes=["e_nmr"])
                tr.op('act', lambda: nc.scalar.activation(out=xp, in_=xp, func=AF.Identity, bias=mv[:, 4:5], scale=mv[:, 3:4]),
                      reads=[kxp, "e_rstd", "e_nmr"], writes=[kxp])
                tr.op('dve', lambda: nc.vector.tensor_tensor(out=xp, in0=xp, in1=self.gB[:], op=ALU.mult),
                      reads=[kxp, "e_gB"], writes=[kxp])
                tr.op('dve', lambda: nc.vector.tensor_tensor(out=xp, in0=xp, in1=self.bB[:], op=ALU.add),
                      reads=[kxp, "e_bB"], writes=[kxp])
                if xdst is not None:
                    tr.dma(ST, xdst, xp, xsem, reads=[kxp], writes=[xkey] if xkey else [])
                if hdst is None:
                    return
                rb = self.xb.next()
                xb = self.xb.t[rb]
                tr.op('act', lambda: nc.scalar.copy(out=xb[:], in_=xp), reads=[kxp], writes=[("e_xb", rb)])
                pA = ps[6][:].bitcast(BF16)
                pB = ps[7][:].bitcast(BF16)

                def tp():
                    for k in range(16):
                        dst = (pA if k < 8 else pB)[:, (k % 8) * 128:(k % 8 + 1) * 128]
                        ins = nc.tensor.transpose(dst, xb[:, k * 128:(k + 1) * 128], identb[:])
                    return ins
                tr.op('pe', tp, reads=[("e_xb", rb), "identb"], writes=[psk[6], psk[7]])
                vc = vcol("valid", g)
                ti = self.tv.next()
                tv = self.tv.t[ti]
                tr.op('dve', lambda: nc.vector.tensor_scalar(out=tv[:, 0:16], in0=S, scalar1=vc, scalar2=None, op0=ALU.mult),
                      reads=["vecs", "modT"], writes=[("e_tv", ti, 0)])
                tr.op('dve', lambda: nc.vector.tensor_scalar(out=tv[:, 16:32], in0=T, scalar1=vc, scalar2=None, op0=ALU.mult),
                      reads=["vecs", "modT"], writes=[("e_tv", ti, 1)])
                rh = self.hts.next()
                hts = self.hts.t[rh]
                for k in range(16):
                    pp = pA if k < 8 else pB
                    tr.op('act', lambda k=k, pp=pp: nc.scalar.activation(
                        out=hts[:, k * 128:(k + 1) * 128], in_=pp[:, (k % 8) * 128:(k % 8 + 1) * 128], func=AF.Identity,
                        bias=tv[:, 16 + k:17 + k], scale=tv[:, k:k + 1]),
                        reads=[psk[6 + k // 8], ("e_tv", ti, 0), ("e_tv", ti, 1)], writes=[("e_hts", rh, k)])
                tr.dma(ST, hdst, hts[:].rearrange("p (k t) -> p k t", t=128), self.hts.sem[rh],
                       reads=[("e_hts", rh, k) for k in range(16)], writes=[hkey] if hkey else [])

        def S_of(l, which, seg):
            return modT[:, l, 16 + 32 * which:32 + 32 * which, seg]

        def T_of(l, which, seg):
            return modT[:, l, 32 * which:16 + 32 * which, seg]

        def seg_of(g):
            return 0 if g < 44 else 1

        def phase_modpre():
            with contextlib.ExitStack() as st:
                W = Ring(tr, st, "m_w", 3, [128, 8192], BF16)
                scT = sb(st, "m_scT", [128, 32], BF16)
                G3 = Ring(tr, st, "m_g3", 2, [3, 512], F32)
                gbo = Ring(tr, st, "m_gbo", 2, [128, 512], F32)
                self_sel = sb(st, "m_sel", [3, 256], F32)
                msem = tr.alloc_sem()
                tr.dma('sp', self_sel[:], sel_d[:, :], msem, writes=["m_sel"])
                tr.op('act', lambda: nc.scalar.activation(out=scT[:], in_=vcol("cT", 0, 32), func=AF.Silu),
                      reads=["vecs"], writes=["m_scT"])
                wst = Stream(W, [(l, i) for l in range(DEPTH) for i in range(24)], 2,
                             lambda slot, it: tr.dma('sp', W.t[slot][:], ws["mod"][it[0], it[1]], W.sem[slot],
                                                     reads=wkeys[("mod", it[0])], writes=[("m_w", slot)]))
                epi = Epi(st, ROW["ln_in_g"], ROW["ln_in_b"])
                xr = Ring(tr, st, "p_x", 3, [128, D], F32)
                blocks = [base + vb for base, n in SEGS for vb in range(*rng_in(0, n))]
                xs = Stream(xr, blocks, 2, lambda slot, g: tr.dma('sp', xr.t[slot][:], xin[g * 128:(g + 1) * 128, :],
                                                                 xr.sem[slot], writes=[("p_x", slot)]))

                def mod_layer(l):
                    pm = ps[0]
                    for i in range(16):
                        w = wst.get()
                        wt = W.t[w]
                        wv = wt[:].rearrange("p (k c) -> p k c", c=512)

                        def mm(i=i, wv=wv):
                            for m in range(4):
                                n = 4 * i + m
                                for k in range(16):
                                    ins = nc.tensor.matmul(pm[:, 2 * n:2 * n + 2], lhsT=wv[:, k, m * 128:(m + 1) * 128],
                                                           rhs=scT[:, 2 * k:2 * k + 2], start=(k == 0), stop=(k == 15))
                            return ins
                        tr.op('pe', mm, reads=[("m_w", w), "m_scT"], writes=[psk[0]])
                        yield
                    tr.op('dve', lambda l=l: nc.vector.tensor_tensor(
                        out=modT[:, l], in0=pm[:, 0:128].rearrange("p (n b) -> p n b", b=2),
                        in1=vcol(f"bmod{l}", 0, 64).unsqueeze(2).to_broadcast([128, 64, 2]), op=ALU.add),
                        reads=[psk[0], "vecs"], writes=["modT"])
                    for wh in range(2):
                        tr.op('dve', lambda l=l, wh=wh: nc.vector.tensor_scalar(
                            out=modT[:, l, 16 + 32 * wh:32 + 32 * wh, :], in0=modT[:, l, 16 + 32 * wh:32 + 32 * wh, :],
                            scalar1=1.0, scalar2=None, op0=ALU.add), reads=["modT"], writes=["modT"])
                    for i in range(8):
                        wh, cg = i // 4, i % 4
                        w = wst.get()
                        wt = W.t[w]
                        wv = wt[:].rearrange("p (k c) -> p k c", c=512)
                        pg = ps[1 + (i % 2)]

                        def mm2(wv=wv, pg=pg):
                            for k in range(16):
                                ins = nc.tensor.matmul(pg[0:2, :], lhsT=scT[:, 2 * k:2 * k + 2], rhs=wv[:, k, :],
                                                       start=(k == 0), stop=(k == 15))
                            return ins
                        tr.op('pe', mm2, reads=[("m_w", w), "m_scT"], writes=[psk[1 + (i % 2)]])
                        gi = G3.next()
                        g3 = G3.t[gi]
                        tr.op('act', lambda g3=g3, pg=pg: nc.scalar.copy(out=g3[0:2, :], in_=pg[0:2, :]),
                              reads=[psk[1 + (i % 2)]], writes=[("m_g3", gi, 0)])
                        rr = ROW[f"bg{wh + 1}{l}"]
                        tr.dma('sp', g3[2:3, :], rows_d[rr:rr + 1, cg * 512:(cg + 1) * 512], G3.sem[gi],
                               writes=[("m_g3", gi, 1)])
                        for seg in range(2):
                            pq = ps[3 + seg]
                            tr.op('pe', lambda pq=pq, g3=g3, seg=seg: nc.tensor.matmul(
                                pq[:, :], lhsT=self_sel[:, seg * 128:(seg + 1) * 128], rhs=g3[:, :], start=True, stop=True),
                                reads=[("m_g3", gi, 0), ("m_g3", gi, 1), "m_sel"], writes=[psk[3 + seg]])
                            oi = gbo.next()
                            tr.op('act', lambda oi=oi, pq=pq: nc.scalar.copy(out=gbo.t[oi][:], in_=pq[:, :]),
                                  reads=[psk[3 + seg]], writes=[("m_gbo", oi)])
                            tr.dma(ST, GBS[l * 4 + wh * 2 + seg, :, cg * 512:(cg + 1) * 512], gbo.t[oi][:], gbo.sem[oi],
                                   reads=[("m_gbo", oi)], writes=[("GBS", l, wh, seg, cg)])
                        yield
                for _ in mod_layer(0):
                    pass
                g1 = mod_layer(1)
                for bi, g in enumerate(blocks):
                    seg = seg_of(g)
                    r = xs.get()
                    epi.run(xr.t[r][:], ("p_x", r), xr.sem[r], g, X0[g * 128:(g + 1) * 128, :],
                            H1T[:, :, 1 + g * 128:1 + (g + 1) * 128], S=S_of(0, 0, seg), T=T_of(0, 0, seg),
                            xkey=("X0", g), hkey=("H1T", g))
                    if bi % 2 == 1:
                        next(g1, None)
                for _ in g1:
                    pass
                phase_end()
                tr.free_sem(msem)

        def phase_p1(l):
            with contextlib.ExitStack() as st:
                W = Ring(tr, st, "a_w", 3, [128, 8192], BF16)
                HT = Ring(tr, st, "a_h", 2, [128, 16 * 512], BF16)
                SG = Ring(tr, st, "a_sg", 4, [128, 2048], BF16)
                VS = Ring(tr, st, "a_vs", 2, [128, 4 * 520], BF16)
                SIG = Ring(tr, st, "a_sig", 2, [128, 512], F32, dma=False)
                bvf = sb(st, "a_bvf", [1, 1024], F32)
                bvb = sb(st, "a_bvb", [1, 1024], BF16)
                s0 = tr.alloc_sem()
                rr = ROW[f"bv{l}"]
                tr.dma('sp', bvf[:], rows_d[rr:rr + 1, 0:1024], s0, writes=["a_bvf"])
                tr.op('act', lambda: nc.scalar.copy(out=bvb[:], in_=bvf[:]), reads=["a_bvf"], writes=["a_bvb"])
                for i in range(2):
                    tr.op('dve', lambda i=i: nc.vector.memset(VS.t[i][:], 1.0), writes=[("a_vs", i)])
                pi = [0]

                def bank():
                    pi[0] = (pi[0] + 1) % 6
                    return pi[0]
                tl = [(base + vb0, nb) for base, n in SEGS for vb0, nb in tiles(*rng_in(l, n), 4)]

                def load_h(slot, it):
                    g0_, nb_ = it
                    tr.dma('sp', HT.t[slot][:].rearrange("p (k t) -> p k t", t=512)[:, :, 0:nb_ * 128],
                           H1T[:, :, 1 + g0_ * 128:1 + (g0_ + nb_) * 128], HT.sem[slot],
                           reads=[("H1T", g0_ + j) for j in range(nb_)], writes=[("a_h", slot)])
                hst = Stream(HT, tl, 1, load_h)

                def sub_range(g0_, nb_):
                    base_, n_ = SEGS[0] if g0_ < 44 else SEGS[1]
                    lo_m, hi_m = rng_mix(l, n_)
                    return max(0, base_ + lo_m - g0_), min(nb_, base_ + hi_m - g0_)

                def tile_list(g0_, nb_):
                    ja_, jb_ = sub_range(g0_, nb_)
                    return [i for i in range(18) if jb_ > ja_ or (2 <= i < 10)]
                wst = Stream(W, [i for g0_, nb_ in tl for i in tile_list(g0_, nb_)], 2,
                             lambda slot, i: tr.dma('sp', W.t[slot][:], ws["in"][l, i], W.sem[slot],
                                                    reads=wkeys[("in", l)], writes=[("a_w", slot)]))
                if True:
                    for g0, nb in tl:
                        N = nb * 128
                        t0 = g0 * 128
                        hr = hst.get()
                        hT = HT.t[hr][:].rearrange("p (k t) -> p k t", t=512)
                        kh = ("a_h", hr)
                        ja, jb = sub_range(g0, nb)
                        for i in tile_list(g0, nb):
                            w = wst.get()
                            wv = W.t[w][:].rearrange("p (k c) -> p k c", c=512)
                            kw = ("a_w", w)

                            def fm(pb, m, wv=wv, hT=hT, N=N, c0=0):
                                def f():
                                    for k in range(16):
                                        ins = nc.tensor.matmul(ps[pb][:, 0:N], lhsT=wv[:, k, m * 128:(m + 1) * 128],
                                                               rhs=hT[:, k, c0:c0 + N], start=(k == 0), stop=(k == 15))
                                    return ins
                                tr.op('pe', f, reads=[kw, kh], writes=[psk[pb]])
                            if i < 4:
                                si = SG.next()
                                sg = SG.t[si]
                                qa, qb = (ja, jb) if i < 2 else (0, nb)
                                nq = qb - qa
                                for m in range(4):
                                    pb = bank()
                                    fm(pb, m, N=nq * 128, c0=qa * 128)
                                    o = sg[:, 0:nq * 512].rearrange("p (b m t) -> p b m t", m=4, t=128)[:, :, m, :]
                                    src = ps[pb][:, 0:nq * 128].rearrange("p (b t) -> p b t", t=128)
                                    bc = vcol(f"b_in{l}", 4 * i + m)
                                    if i < 2:
                                        tr.op('dve', lambda o=o, src=src, bc=bc: nc.vector.tensor_scalar(
                                            out=o, in0=src, scalar1=bc, scalar2=0.125, op0=ALU.add, op1=ALU.mult),
                                            reads=[psk[pb], "vecs"], writes=[("a_sg", si, m)])
                                    else:
                                        tr.op('act', lambda o=o, src=src, bc=bc: nc.scalar.activation(
                                            out=o, in_=src, func=AF.Identity, bias=bc, scale=1.0),
                                            reads=[psk[pb], "vecs"], writes=[("a_sg", si, m)])
                                dst = (QT if i < 2 else KT)[g0 + qa:g0 + qb, :, (i % 2) * 512:(i % 2 + 1) * 512]
                                tr.dma(ST, dst.rearrange("b p c -> p b c"),
                                       sg[:, 0:nq * 512].rearrange("p (b c) -> p b c", c=512), SG.sem[si],
                                       reads=[("a_sg", si, m) for m in range(4)],
                                       writes=[("QT" if i < 2 else "KT", g0 + j, i % 2) for j in range(qa, qb)])
                            elif i < 6:
                                vi = VS.next()
                                vs = VS.t[vi]
                                hh = i - 4
                                for j in range(nb):
                                    pb = bank()

                                    def f(pb=pb, j=j, wv=wv, hT=hT):
                                        for k in range(16):
                                            nc.tensor.matmul(ps[pb][:, :], lhsT=hT[:, k, j * 128:(j + 1) * 128], rhs=wv[:, k, :],
                                                             start=(k == 0), stop=False)
                                        return nc.tensor.matmul(ps[pb][:, :], lhsT=onesrow[0:1, :],
                                                                rhs=bvb[0:1, hh * 512:(hh + 1) * 512], start=False, stop=True)
                                    tr.op('pe', f, reads=[kw, kh, "a_bvb", "onesrow"], writes=[psk[pb]])
                                    o = vs[:, j * 520:(j + 1) * 520].rearrange("p (h e) -> p h e", e=65)[:, :, 0:64]
                                    tr.op('act', lambda o=o, pb=pb: nc.scalar.copy(
                                        out=o, in_=ps[pb][:, :].rearrange("p (h e) -> p h e", e=64)),
                                        reads=[psk[pb]], writes=[("a_vs", vi, j)])
                                tr.dma(ST, VV[g0:g0 + nb, :, hh * 520:(hh + 1) * 520].rearrange("b p c -> p b c"),
                                       vs[:, 0:nb * 520].rearrange("p (b c) -> p b c", c=520), VS.sem[vi],
                                       reads=[("a_vs", vi, j) for j in range(nb)] + [("a_vs", vi)],
                                       writes=[("VV", g0 + j, hh) for j in range(nb)])
                            elif i < 10:
                                si = SG.next()
                                sg = SG.t[si]
                                ii = i - 6
                                for r in range(2):
                                    pv = bank()
                                    fm(pv, r)
                                    pg = bank()
                                    fm(pg, 2 + r)
                                    gi = SIG.next()
                                    sig = SIG.t[gi]
                                    bg = vcol(f"b_in{l}", 4 * i + 2 + r)
                                    bvv = vcol(f"b_in{l}", 4 * i + r)
                                    tr.op('act', lambda sig=sig, pg=pg, bg=bg: nc.scalar.activation(
                                        out=sig[:, 0:N], in_=ps[pg][:, 0:N], func=AF.Sigmoid, bias=bg, scale=1.0),
                                        reads=[psk[pg], "vecs"], writes=[("a_sig", gi)])
                                    tr.op('dve', lambda sig=sig, pv=pv, bvv=bvv, r=r, sg=sg: nc.vector.scalar_tensor_tensor(
                                        out=sg[:, r * N:(r + 1) * N], in0=ps[pv][:, 0:N], scalar=bvv, in1=sig[:, 0:N],
                                        op0=ALU.add, op1=ALU.mult),
                                        reads=[psk[pv], ("a_sig", gi), "vecs"], writes=[("a_sg", si, r)])
                                tr.dma(ST, UT[:, 2 * ii:2 * ii + 2, 16 + t0:16 + t0 + N],
                                       sg[:, 0:2 * N].rearrange("p (m t) -> p m t", t=N), SG.sem[si],
                                       reads=[("a_sg", si, r) for r in range(2)],
                                       writes=[("UT", g0 + j, ii) for j in range(nb)])
                            else:
                                si = SG.next()
                                sg = SG.t[si]
                                ii = i - 10
                                Ng = (jb - ja) * 128
                                for m in range(4):
                                    pb = bank()
                                    fm(pb, m, N=Ng, c0=ja * 128)
                                    bc = vcol(f"b_in{l}", 4 * i + m)
                                    tr.op('act', lambda sg=sg, pb=pb, bc=bc, m=m: nc.scalar.activation(
                                        out=sg[:, m * Ng:(m + 1) * Ng], in_=ps[pb][:, 0:Ng], func=AF.Sigmoid, bias=bc, scale=1.0),
                                        reads=[psk[pb], "vecs"], writes=[("a_sg", si, m)])
                                pos = 8 * ii if ii < 4 else 8 * (ii - 4) + 4
                                tr.dma(ST, GT[:, pos:pos + 4, t0 + ja * 128:t0 + jb * 128],
                                       sg[:, 0:4 * Ng].rearrange("p (m t) -> p m t", t=Ng), SG.sem[si],
                                       reads=[("a_sg", si, m) for m in range(4)],
                                       writes=[("GT", g0 + j, pos) for j in range(ja, jb)])
                phase_end()
                tr.free_sem(s0)

        def phase_p2(l):
            from collections import deque
            with contextlib.ExitStack() as st:
                U = sb(st, "b_U", [128, NH * 896], BF16)
                Qm = sb(st, "b_qm", [80, NTOK], BF16)
                Kp = sb(st, "b_kp", [80, 1024], BF16)
                s0 = tr.alloc_sem()
                tr.dma('pool', U[:], utab_d[l], s0, writes=["b_U"])
                for pofs in (0, 64):
                    tr.dma('pool', Qm[pofs:pofs + 16, :], qmask_d[:, :], s0, writes=[("b_qm", pofs)])
                    tr.dma('pool', Kp[pofs:pofs + 16, :], kpat_d[:, :], s0, writes=[("b_kp", pofs)])
                for piece in range(7):
                    tr.op('act', lambda piece=piece: nc.scalar.activation(
                        out=U[:, piece * 2048:(piece + 1) * 2048], in_=U[:, piece * 2048:(piece + 1) * 2048], func=AF.Exp),
                        reads=["b_U"], writes=["b_U"])
                Uv = U[:].rearrange("p (h s c) -> p h s c", s=7, c=128)
                XR = Ring(tr, st, "b_xr", 4, [128, 512], BF16, dma=False)
                Qr = Ring(tr, st, "b_q", 3, [128, 1024], BF16)
                Kr = Ring(tr, st, "b_k", 8, [128, 1024], BF16)
                Vr = Ring(tr, st, "b_v", 8, [128, 1040], BF16)
                PT = Ring(tr, st, "b_pt", 4, [128, 512], BF16, dma=False)
                AO = Ring(tr, st, "b_ao", 2, [128, 1024], BF16, dma=False)
                ATS = Ring(tr, st, "b_ats", 2, [128, 1024], BF16)
                RC = Ring(tr, st, "b_rc", 2, [128, 4], F32, dma=False)
                LOOK = 3
                SB = [0, 1, 2, 6, 7]
                scnt = [0]
                mcnt = [0]
                for base, n in SEGS:
                    lo, hi = rng_mix(l, n)
                    loaded = set()
                    items = []
                    for vb in range(lo, hi):
                        g = base + vb
                        cs = list(range(g - 2, g + 3))
                        if vb == HALO:
                            cs.append(g + 3)
                        if vb == HALO + n - 1:
                            cs.insert(0, g - 3)
                        bst = {"g": g, "cs": cs, "edge": vb in (HALO, HALO + n - 1)}
                        for h in range(NH):
                            for gi_, grp in enumerate((cs[0:4], cs[4:])):
                                items.append({"b": bst, "h": h, "gi": gi_, "grp": grp})

                    def start(it):
                        bst = it["b"]
                        g, cs, h, grp = bst["g"], bst["cs"], it["h"], it["grp"]
                        if h == 0 and it["gi"] == 0:
                            for c in cs:
                                if c not in loaded:
                                    loaded.add(c)
                                    tr.dma('sp', Kr.t[c % 8][:], KT[c], Kr.sem[c % 8], reads=[("KT", c, 0), ("KT", c, 1)],
                                           writes=[("b_k", c % 8)])
                                    tr.dma('sp', Vr.t[c % 8][:], VV[c], Vr.sem[c % 8], reads=[("VV", c, 0), ("VV", c, 1)],
                                           writes=[("b_v", c % 8)])
                            qi = Qr.next()
                            tr.dma('sp', Qr.t[qi][:], QT[g], Qr.sem[qi], reads=[("QT", g, 0), ("QT", g, 1)], writes=[("b_q", qi)])
                            bst["qi"] = qi
                            bst["ai"] = AO.next()
                        qi = bst["qi"]
                        q = Qr.t[qi]
                        t, p0 = h // 2, 64 * (h % 2)
                        scnt[0] = (scnt[0] + 1) % len(SB)
                        sbk = SB[scnt[0]]
                        it["sbk"] = sbk
                        S = ps[sbk]

                        def sc():
                            for ci, c in enumerate(grp):
                                o = S[:, ci * 128:(ci + 1) * 128]
                                need_mask = bst["edge"] or abs(c - g) >= 2
                                ins = nc.tensor.matmul(o, lhsT=Kr.t[c % 8][p0:p0 + 64, t * 128:(t + 1) * 128],
                                                       rhs=q[p0:p0 + 64, t * 128:(t + 1) * 128], start=True, stop=not need_mask)
                                if need_mask:
                                    ins = nc.tensor.matmul(o, lhsT=Kp[p0:p0 + 16, (c % 8) * 128:(c % 8 + 1) * 128],
                                                           rhs=Qm[p0:p0 + 16, g * 128:(g + 1) * 128], start=False, stop=True)
                            return ins
                        tr.op('pe', sc, reads=[("b_k", c % 8) for c in grp] + [("b_q", qi), ("b_qm", 0), ("b_qm", 64), ("b_kp", 0),
                                                 ("b_kp", 64)],
                              writes=[psk[sbk]])

                    def finish(it):
                        bst = it["b"]
                        g, cs, h, grp, gi_ = bst["g"], bst["cs"], it["h"], it["grp"], it["gi"]
                        sbk = it["sbk"]
                        S = ps[sbk]
                        hq, hl = h // 4, h % 4
                        ob = 3 + (hq % 2)
                        O = ps[ob]
                        ai = bst["ai"]
                        ao = AO.t[ai]
                        pi_ = PT.next()
                        pt = PT.t[pi_]
                        ncol = len(grp) * 128
                        xi = XR.next()
                        xr_ = XR.t[xi]
                        tr.op('act', lambda: nc.scalar.activation(out=xr_[:, 0:ncol], in_=S[:, 0:ncol], func=AF.Exp),
                              reads=[psk[sbk]], writes=[("b_xr", xi)])
                        s0_ = grp[0] - g + 3
                        esl = Uv[:, h, s0_:s0_ + len(grp), :]
                        mcnt[0] += 1
                        if True:
                            tr.op('dve', lambda: nc.vector.tensor_tensor(
                                out=pt[:, 0:ncol].rearrange("p (s c) -> p s c", c=128),
                                in0=xr_[:, 0:ncol].rearrange("p (s c) -> p s c", c=128), in1=esl, op=ALU.mult),
                                reads=[("b_xr", xi), "b_U"], writes=[("b_pt", pi_)])

                        def pv():
                            for ci, c in enumerate(grp):
                                ins = nc.tensor.matmul(O[:, hl * 65:(hl + 1) * 65], lhsT=pt[:, ci * 128:(ci + 1) * 128],
                                                       rhs=Vr.t[c % 8][:, h * 65:(h + 1) * 65],
                                                       start=(gi_ == 0 and ci == 0), stop=(gi_ == 1 and ci == len(grp) - 1))
                            return ins
                        tr.op('pe', pv, reads=[("b_pt", pi_)] + [("b_v", c % 8) for c in grp], writes=[psk[ob]])
                        if gi_ == 1 and hl == 3:
                            ri = RC.next()
                            rc = RC.t[ri]
                            Ov = O[:, 0:260].rearrange("p (h e) -> p h e", e=65)
                            tr.op('dve', lambda: nc.vector.reciprocal(out=rc[:, :], in_=Ov[:, :, 64]),
                                  reads=[psk[ob]], writes=[("b_rc", ri)])
                            tr.op('dve', lambda: nc.vector.tensor_tensor(
                                out=ao[:, hq * 256:(hq + 1) * 256].rearrange("p (h e) -> p h e", e=64), in0=Ov[:, :, 0:64],
                                in1=rc[:, :].unsqueeze(2).to_broadcast([128, 4, 64]), op=ALU.mult),
                                reads=[psk[ob], ("b_rc", ri)], writes=[("b_ao", ai, hq)])
                        if gi_ == 1 and h == NH - 1:
                            pT = ps[5][:].bitcast(BF16)

                            def tp():
                                for t in range(8):
                                    ins = nc.tensor.transpose(pT[:, t * 128:(t + 1) * 128], ao[:, t * 128:(t + 1) * 128], identb[:])
                                return ins
                            tr.op('pe', tp, reads=[("b_ao", ai, x) for x in range(4)] + ["identb"], writes=[psk[5]])
                            ti = ATS.next()
                            tr.op('dve', lambda: nc.vector.tensor_copy(out=ATS.t[ti][:], in_=pT), reads=[psk[5]],
                                  writes=[("b_ats", ti)])
                            tr.dma(ST, AT[:, :, g * 128:(g + 1) * 128], ATS.t[ti][:].rearrange("p (k t) -> p k t", t=128),
                                   ATS.sem[ti], reads=[("b_ats", ti)], writes=[("AT", g)])
                    pend = deque()
                    for it in items:
                        start(it)
                        pend.append(it)
                        if len(pend) > LOOK:
                            finish(pend.popleft())
                    while pend:
                        finish(pend.popleft())
                phase_end()
                tr.free_sem(s0)

        def phase_p3(l):
            with contextlib.ExitStack() as st:
                Dg = sb(st, "c_dg", [128, 248 * 128], BF16)
                onesf = sb(st, "c_ones", [128, 128], F32)
                tr.op('dve', lambda: nc.vector.memset(onesf[:], 1.0 / 1024.0), writes=["c_ones"])
                for mj in range(248):
                    tr.op('dve', lambda mj=mj: nc.vector.tensor_scalar(
                        out=Dg[:, mj * 128:(mj + 1) * 128], in0=identb[:], scalar1=vcol(f"cw{l}", mj), scalar2=None, op0=ALU.mult),
                        reads=["identb", "vecs"], writes=[("c_dg", mj)])
                dgk = [("c_dg", mj) for mj in range(248)]
                UTr = Ring(tr, st, "c_ut", 2, [128, 8 * 544], BF16)
                uc = [sb(st, f"c_uc{i}", [128, 8 * 512], F32) for i in range(2)]
                sq = [sb(st, f"c_sq{i}", [128, 8 * 512], F32) for i in range(2)]
                tcount = [0]
                pend = [None]
                CTr = Ring(tr, st, "c_ct", 2, [128, 8 * 512], BF16)
                msb = [sb(st, f"c_msb{i}", [128, 512], F32) for i in range(2)]
                var = [sb(st, f"c_var{i}", [128, 512], F32) for i in range(2)]
                pi = [0]
                tl = [(base + vb0, nb) for base, n in SEGS for vb0, nb in tiles(*rng_mix(l, n), 4)]

                def load_u(slot, it):
                    g0_, nb_ = it
                    N_ = nb_ * 128
                    tr.dma('sp', UTr.t[slot][:].rearrange("p (m t) -> p m t", t=544)[:, :, 0:N_ + 32],
                           UT[:, :, g0_ * 128:g0_ * 128 + N_ + 32], UTr.sem[slot],
                           reads=[("UT", g0_ + j, ii) for j in range(-1, nb_ + 1) for ii in range(4)], writes=[("c_ut", slot)])
                ust = Stream(UTr, tl, 1, load_u)
                if True:
                    for g0, nb in tl:
                        N = nb * 128
                        t0 = g0 * 128
                        ui = ust.get()
                        ut = UTr.t[ui][:].rearrange("p (m t) -> p m t", t=544)
                        ku = ("c_ut", ui)
                        pieces = [(0, 16, g0 - 1)] + [(16 + 128 * j, 16 + 128 * (j + 1), g0 + j) for j in range(nb)] + \
                                 [(16 + N, 32 + N, g0 + nb)]
                        for a, b, gb in pieces:
                            tr.op('dve', lambda a=a, b=b, gb=gb: nc.vector.tensor_scalar(
                                out=ut[:, :, a:b], in0=ut[:, :, a:b], scalar1=vcol("valid", gb), scalar2=None, op0=ALU.mult),
                                reads=[ku, "vecs"], writes=[ku])
                        par = tcount[0] % 2
                        tcount[0] += 1
                        ucv = uc[par][:].rearrange("p (m t) -> p m t", t=512)
                        sqv = sq[par][:].rearrange("p (m t) -> p m t", t=512)
                        for m in range(8):
                            pi[0] = (pi[0] + 1) % 3
                            pb = pi[0]

                            def cv(m=m, pb=pb):
                                for j in range(31):
                                    ins = nc.tensor.matmul(ps[pb][:, 0:N], lhsT=Dg[:, (m * 31 + j) * 128:(m * 31 + j + 1) * 128],
                                                           rhs=ut[:, m, j + 1:j + 1 + N], start=(j == 0), stop=(j == 30))
                                return ins
                            tr.op('pe', cv, reads=[ku] + dgk[m * 31:(m + 1) * 31], writes=[psk[pb]])
                            tr.op('act', lambda m=m, pb=pb: nc.scalar.activation(
                                out=ucv[:, m, 0:N], in_=ps[pb][:, 0:N], func=AF.Identity, bias=vcol(f"conv_b{l}", m), scale=1.0),
                                reads=[psk[pb], "vecs"], writes=[("c_uc", par, m)])
                            tr.op('act', lambda m=m, pb=pb: nc.scalar.activation(
                                out=sqv[:, m, 0:N], in_=ps[pb][:, 0:N], func=AF.Square, bias=vcol(f"conv_b{l}", m), scale=1.0),
                                reads=[psk[pb], "vecs"], writes=[("c_sq", par, m)])

                        def stat(src, pb):
                            def f():
                                for m in range(8):
                                    ins = nc.tensor.matmul(ps[pb][:, 0:N], lhsT=onesf[:], rhs=src[:, m, 0:N], start=(m == 0), stop=(m == 7))
                                return ins
                            return f
                        pm_, pq_ = 3 + 2 * par, 4 + 2 * par
                        tr.op('pe', stat(ucv, pm_), reads=[("c_uc", par, m) for m in range(8)] + ["c_ones"], writes=[psk[pm_]])
                        tr.op('pe', stat(sqv, pq_), reads=[("c_sq", par, m) for m in range(8)] + ["c_ones"], writes=[psk[pq_]])

                        def finish(par=par, ucv=ucv, N=N, t0=t0, g0=g0, nb=nb, pm_=pm_, pq_=pq_):
                            msb_, var_ = msb[par], var[par]
                            tr.op('act', lambda: nc.scalar.copy(out=msb_[:, 0:N], in_=ps[pm_][:, 0:N]), reads=[psk[pm_]],
                                  writes=[("c_msb", par)])
                            tr.op('dve', lambda: nc.vector.tensor_tensor(out=var_[:, 0:N], in0=msb_[:, 0:N], in1=msb_[:, 0:N], op=ALU.mult),
                                  reads=[("c_msb", par)], writes=[("c_var", par)])
                            tr.op('dve', lambda: nc.vector.tensor_tensor(out=var_[:, 0:N], in0=ps[pq_][:, 0:N], in1=var_[:, 0:N],
                                                                         op=ALU.subtract),
                                  reads=[psk[pq_], ("c_var", par)], writes=[("c_var", par)])
                            tr.op('act', lambda: nc.scalar.activation(out=var_[:, 0:N], in_=var_[:, 0:N], func=AF.Sqrt, bias=EPS, scale=1.0),
                                  reads=[("c_var", par)], writes=[("c_var", par)])
                            tr.op('dve', lambda: nc.vector.reciprocal(out=var_[:, 0:N], in_=var_[:, 0:N]), reads=[("c_var", par)],
                                  writes=[("c_var", par)])
                            ci = CTr.next()
                            ct = CTr.t[ci][:].rearrange("p (m t) -> p m t", t=512)
                            for m in range(8):
                                tr.op('dve', lambda m=m: nc.vector.tensor_tensor(out=ucv[:, m, 0:N], in0=ucv[:, m, 0:N], in1=msb_[:, 0:N],
                                                                                 op=ALU.subtract),
                                      reads=[("c_uc", par, m), ("c_msb", par)], writes=[("c_uc", par, m)])
                                tr.op('dve', lambda m=m: nc.vector.tensor_tensor(out=ucv[:, m, 0:N], in0=ucv[:, m, 0:N], in1=var_[:, 0:N],
                                                                                 op=ALU.mult),
                                      reads=[("c_uc", par, m), ("c_var", par)], writes=[("c_uc", par, m)])
                                tr.op('act', lambda m=m: nc.scalar.activation(
                                    out=ct[:, m, 0:N], in_=ucv[:, m, 0:N], func=AF.Silu, bias=vcol(f"cln_b{l}", m),
                                    scale=vcol(f"cln_g{l}", m)),
                                    reads=[("c_uc", par, m), "vecs"], writes=[("c_ct", ci, m)])
                            tr.dma(ST, CT[:, :, t0:t0 + N], ct[:, :, 0:N], CTr.sem[ci], reads=[("c_ct", ci, m) for m in range(8)],
                                   writes=[("CT", g0 + j) for j in range(nb)])
                        if pend[0] is not None:
                            pend[0]()
                        pend[0] = finish
                if pend[0] is not None:
                    pend[0]()
                phase_end()

        def phase_p4(l):
            with contextlib.ExitStack() as st:
                TN = 4
                NM = TN * 128
                epi = Epi(st, ROW[f"mix_g{l}"], ROW[f"mix_b{l}"])
                W = Ring(tr, st, "d_w", 2, [128, 8192], BF16)
                ATr = Ring(tr, st, "d_at", 1, [128, 8 * NM], BF16)
                CTr = Ring(tr, st, "d_ct", 1, [128, 8 * NM], BF16)
                GTr = Ring(tr, st, "d_gt", 2, [128, 8 * NM], BF16)
                yT = sb(st, "d_yT", [128, 16 * NM], BF16)
                T1 = Ring(tr, st, "d_t1", 2, [128, 512], F32, dma=False)
                T2 = Ring(tr, st, "d_t2", 2, [128, 512], F32, dma=False)
                XP = Ring(tr, st, "d_xp", 2 * TN, [128, D], F32)
                g1B = sb(st, "d_g1B", [128, D], F32)
                gsem = tr.alloc_sem()
                Xsrc = X0 if l == 0 else X1
                xname = "X0" if l == 0 else "X1"
                pi = [0]

                def bank():
                    pi[0] = (pi[0] + 1) % 6
                    return pi[0]
                yv = yT[:].rearrange("p (k t) -> p k t", t=NM)
                tl_all = [(sidx, base + vb0, nb) for sidx, (base, n) in enumerate(SEGS) for vb0, nb in tiles(*rng_mix(l, n), TN)]

                def load_at(slot, it):
                    _, g0_, nb_ = it
                    tr.dma('sp', ATr.t[slot][:].rearrange("p (k t) -> p k t", t=NM)[:, :, 0:nb_ * 128],
                           AT[:, :, g0_ * 128:(g0_ + nb_) * 128], ATr.sem[slot], reads=[("AT", g0_ + j) for j in range(nb_)],
                           writes=[("d_at", slot)])

                def load_ct(slot, it):
                    _, g0_, nb_ = it
                    tr.dma('sp', CTr.t[slot][:].rearrange("p (k t) -> p k t", t=NM)[:, :, 0:nb_ * 128],
                           CT[:, :, g0_ * 128:(g0_ + nb_) * 128], CTr.sem[slot], reads=[("CT", g0_ + j) for j in range(nb_)],
                           writes=[("d_ct", slot)])

                def load_gt(slot, it):
                    _, g0_, nb_, j4_ = it
                    tr.dma('sp', GTr.t[slot][:].rearrange("p (k t) -> p k t", t=NM)[:, :, 0:nb_ * 128],
                           GT[:, 8 * j4_:8 * j4_ + 8, g0_ * 128:(g0_ + nb_) * 128], GTr.sem[slot],
                           reads=[("GT", g0_ + j, 8 * j4_ + o) for j in range(nb_) for o in (0, 4)], writes=[("d_gt", slot)])

                def load_xp(slot, g_):
                    tr.dma('sp', XP.t[slot][:], Xsrc[g_ * 128:(g_ + 1) * 128, :], XP.sem[slot],
                           reads=[(xname, g_)], writes=[("d_xp", slot)])
                ast = Stream(ATr, tl_all, 0, load_at)
                cst = Stream(CTr, tl_all, 0, load_ct)
                gst = Stream(GTr, [(a_, b_, c_, j4) for a_, b_, c_ in tl_all for j4 in range(4)], 1, load_gt)
                xst = Stream(XP, [g0_ + j for _, g0_, nb_ in tl_all for j in range(nb_)], 0, load_xp)
                wst = Stream(W, [(grp_, j4) for _ in tl_all for grp_ in ("ac", "out") for j4 in range(4)], 1,
                             lambda slot, it: tr.dma('sp', W.t[slot][:], ws[it[0]][l, it[1]], W.sem[slot],
                                                     reads=wkeys[(it[0], l)], writes=[("d_w", slot)]))
                cur_seg = [-1]
                pend_epi = [None]
                if True:
                    for sidx, g0, nb in tl_all:
                        if sidx != cur_seg[0]:
                            cur_seg[0] = sidx
                            tr.dma('sp', g1B[:], GBS[l * 4 + 0 + sidx], gsem, reads=[("GBS", l, 0, sidx, cg) for cg in range(4)],
                                   writes=["d_g1B"])
                        N = nb * 128
                        t0 = g0 * 128
                        ai = ast.get()
                        at = ATr.t[ai][:].rearrange("p (k t) -> p k t", t=NM)
                        ci = cst.get()
                        ct = CTr.t[ci][:].rearrange("p (k t) -> p k t", t=NM)
                        xsl = []
                        for j in range(nb):
                            xs_ = xst.get()
                            xsl.append(xs_)
                            tr.op('act', lambda xs_=xs_: nc.scalar.mul(out=XP.t[xs_][:], in_=XP.t[xs_][:], mul=float(ALPHA)),
                                  reads=[("d_xp", xs_)], writes=[("d_xp", xs_)])
                        for j4 in range(4):
                            w = wst.get()
                            wv = W.t[w][:].rearrange("p (a k c) -> p a k c", a=2, c=512)
                            kw = ("d_w", w)
                            gi = gst.get()
                            gt = GTr.t[gi][:].rearrange("p (k t) -> p k t", t=NM)
                            for m in range(4):
                                nn = 4 * j4 + m
                                pa, pc = bank(), bank()
                                for a, pb, src, ks in ((0, pa, at, ("d_at", ai)), (1, pc, ct, ("d_ct", ci))):
                                    def f(a=a, pb=pb, src=src, m=m, wv=wv):
                                        for k in range(8):
                                            ins = nc.tensor.matmul(ps[pb][:, 0:N], lhsT=wv[:, a, k, m * 128:(m + 1) * 128],
                                                                   rhs=src[:, k, 0:N], start=(k == 0), stop=(k == 7))
                                        return ins
                                    tr.op('pe', f, reads=[kw, ks], writes=[psk[pb]])
                                i1, i2 = T1.next(), T2.next()
                                tr.op('dve', lambda i1=i1, pa=pa, m=m, gt=gt: nc.vector.tensor_tensor(
                                    out=T1.t[i1][:, 0:N], in0=ps[pa][:, 0:N], in1=gt[:, m, 0:N], op=ALU.mult),
                                    reads=[psk[pa], ("d_gt", gi)], writes=[("d_t1", i1)])
                                tr.op('dve', lambda i2=i2, pc=pc, m=m, gt=gt: nc.vector.tensor_tensor(
                                    out=T2.t[i2][:, 0:N], in0=ps[pc][:, 0:N], in1=gt[:, 4 + m, 0:N], op=ALU.mult),
                                    reads=[psk[pc], ("d_gt", gi)], writes=[("d_t2", i2)])
                                tr.op('dve', lambda i1=i1, i2=i2, nn=nn: nc.vector.tensor_tensor(
                                    out=yv[:, nn, 0:N], in0=T1.t[i1][:, 0:N], in1=T2.t[i2][:, 0:N], op=ALU.add),
                                    reads=[("d_t1", i1), ("d_t2", i2)], writes=[("d_yT", nn)])
                        ast.prefetch()
                        cst.prefetch()
                        if pend_epi[0] is not None:
                            pend_epi[0]()
                            pend_epi[0] = None
                        for c in range(4):
                            w = wst.get()
                            wv = W.t[w][:].rearrange("p (k c) -> p k c", c=512)
                            kw = ("d_w", w)
                            for j in range(nb):
                                pb = bank()

                                def f(pb=pb, j=j, wv=wv):
                                    for k in range(16):
                                        ins = nc.tensor.matmul(ps[pb][:, :], lhsT=yv[:, k, j * 128:(j + 1) * 128], rhs=wv[:, k, :],
                                                               start=(k == 0), stop=(k == 15))
                                    return ins
                                tr.op('pe', f, reads=[kw] + [("d_yT", nn) for nn in range(16)], writes=[psk[pb]])
                                i1 = T1.next()
                                tr.op('dve', lambda i1=i1, pb=pb, c=c: nc.vector.tensor_tensor(
                                    out=T1.t[i1][:, :], in0=ps[pb][:, :], in1=g1B[:, c * 512:(c + 1) * 512], op=ALU.mult),
                                    reads=[psk[pb], "d_g1B"], writes=[("d_t1", i1)])
                                xp = XP.t[xsl[j]]
                                tr.op('dve', lambda i1=i1, xp=xp, c=c: nc.vector.tensor_tensor(
                                    out=xp[:, c * 512:(c + 1) * 512], in0=xp[:, c * 512:(c + 1) * 512],
                                    in1=T1.t[i1][:, :], op=ALU.add),
                                    reads=[("d_t1", i1), ("d_xp", xsl[j])], writes=[("d_xp", xsl[j])])
                        def do_epi(g0=g0, nb=nb, xsl=xsl, sidx=sidx):
                            for j in range(nb):
                                g = g0 + j
                                epi.run(XP.t[xsl[j]][:], ("d_xp", xsl[j]), XP.sem[xsl[j]], g, XM[g * 128:(g + 1) * 128, :],
                                        H2T[:, :, 1 + g * 128:1 + (g + 1) * 128],
                                        S=S_of(l, 1, sidx), T=T_of(l, 1, sidx),
                                        xkey=("XM", g), hkey=("H2T", g))
                        pend_epi[0] = do_epi
                if pend_epi[0] is not None:
                    pend_epi[0]()
                phase_end()
                tr.free_sem(gsem)

        def phase_p5(l):
            with contextlib.ExitStack() as st:
                last = (l == DEPTH - 1)
                epi = Epi(st, ROW[f"ffn_g{l}"], ROW[f"ffn_b{l}"])
                W = Ring(tr, st, "f_w", 3, [128, 8192], BF16)
                H2 = sb(st, "f_h2", [128, 16 * 514], BF16)
                hsem = tr.alloc_sem()
                ASB = Ring(tr, st, "f_a", 2, [128, 514], BF16, dma=False)
                DG = Ring(tr, st, "f_dg", 2, [128, 3 * 128], BF16, dma=False)
                GEL = Ring(tr, st, "f_gel", 2, [128, 512], F32, dma=False)
                gT = sb(st, "f_gT", [128, NFC * 512], BF16)
                XP = Ring(tr, st, "f_xp", 4, [128, D], F32)
                T1 = Ring(tr, st, "f_t1", 2, [128, 512], F32, dma=False)
                g2B = sb(st, "f_g2B", [128, D], F32)
                gsem = tr.alloc_sem()
                h2 = H2[:].rearrange("p (k t) -> p k t", t=514)
                gv = gT[:].rearrange("p (k t) -> p k t", t=512)
                pi = [0]

                def bank():
                    pi[0] = (pi[0] + 1) % 8
                    return pi[0]
                tl_all = [(sidx, base, base + vb0, nb) for sidx, (base, n) in enumerate(SEGS) for vb0, nb in tiles_bal(*rng_ffn(l, n), 4)]

                def load_w(slot, it):
                    grp_, idx_ = it
                    if grp_ == "up":
                        tr.dma('sp', W.t[slot][:], ws["up"][l, idx_], W.sem[slot], reads=wkeys[("up", l)], writes=[("f_w", slot)])
                    else:
                        tr.dma('sp', W.t[slot][:, 0:5632], ws["down"][l, idx_], W.sem[slot], reads=wkeys[("down", l)],
                               writes=[("f_w", slot)])
                wst = Stream(W, [it for _ in tl_all for it in ([("up", i) for i in range(22)] + [("down", i) for i in range(16)])],
                             2, load_w)

                def load_h2(ti):
                    _, _, g0_, nb_ = tl_all[ti]
                    tr.dma('sp', h2[:, :, 0:nb_ * 128 + 2], H2T[:, :, g0_ * 128:(g0_ + nb_) * 128 + 2], hsem,
                           reads=[("H2T", g0_ + j) for j in range(-1, nb_ + 1)], writes=["f_h2"])
                load_h2(0)
                pend_epi = [None]
                cur_seg = [-1]
                if True:
                    for ti, (sidx, base, g0, nb) in enumerate(tl_all):
                        if sidx != cur_seg[0]:
                            cur_seg[0] = sidx
                            tr.dma('sp', g2B[:], GBS[l * 4 + 2 + sidx], gsem, reads=[("GBS", l, 1, sidx, cg) for cg in range(4)],
                                   writes=["f_g2B"])
                        N = nb * 128
                        t0 = g0 * 128
                        kh = "f_h2"
                        for i in range(22):
                            if i == 6 and pend_epi[0] is not None:
                                pend_epi[0]()
                                pend_epi[0] = None
                            w = wst.get()
                            wv = W.t[w][:].rearrange("p (k c) -> p k c", c=512)
                            kw = ("f_w", w)
                            for r in range(2):
                                m = 2 * i + r
                                if m >= NFC:
                                    continue
                                pa, pbb, pcv, pe_ = bank(), bank(), bank(), bank()

                                def fa(pa=pa, pe_=pe_, r=r, wv=wv):
                                    for k in range(16):
                                        nc.tensor.matmul(ps[pa][:, 0:N], lhsT=wv[:, k, r * 128:(r + 1) * 128], rhs=h2[:, k, 1:N + 1],
                                                         start=(k == 0), stop=(k == 15))
                                    for k in range(16):
                                        ins = nc.tensor.matmul(ps[pe_][:, 0:2], lhsT=wv[:, k, r * 128:(r + 1) * 128],
                                                               rhs=h2[:, k, 0:N + 2:N + 1], start=(k == 0), stop=(k == 15))
                                    return ins
                                tr.op('pe', fa, reads=[kw, kh], writes=[psk[pa], psk[pe_]])

                                def fb(pbb=pbb, r=r, wv=wv):
                                    for k in range(16):
                                        ins = nc.tensor.matmul(ps[pbb][:, 0:N], lhsT=wv[:, k, (2 + r) * 128:(3 + r) * 128],
                                                               rhs=h2[:, k, 1:N + 1], start=(k == 0), stop=(k == 15))
                                    return ins
                                tr.op('pe', fb, reads=[kw, kh], writes=[psk[pbb]])
                                ai = ASB.next()
                                asb = ASB.t[ai]
                                tr.op('act', lambda asb=asb, pa=pa: nc.scalar.copy(out=asb[:, 1:N + 1], in_=ps[pa][:, 0:N]),
                                      reads=[psk[pa]], writes=[("f_a", ai, 0)])
                                tr.op('act', lambda asb=asb, pe_=pe_: nc.scalar.copy(out=asb[:, 0:N + 2:N + 1], in_=ps[pe_][:, 0:2]),
                                      reads=[psk[pe_]], writes=[("f_a", ai, 1)])
                                di = DG.next()
                                dg = DG.t[di]
                                for jj in range(3):
                                    tr.op('dve', lambda jj=jj, dg=dg, m=m: nc.vector.tensor_scalar(
                                        out=dg[:, jj * 128:(jj + 1) * 128], in0=identb[:], scalar1=vcol(f"fcw{l}", jj * NFC + m),
                                        scalar2=None, op0=ALU.mult), reads=["identb", "vecs"], writes=[("f_dg", di, jj)])

                                def fc(pcv=pcv, dg=dg, asb=asb):
                                    for jj in range(3):
                                        ins = nc.tensor.matmul(ps[pcv][:, 0:N], lhsT=dg[:, jj * 128:(jj + 1) * 128], rhs=asb[:, jj:jj + N],
                                                               start=(jj == 0), stop=(jj == 2))
                                    return ins
                                tr.op('pe', fc, reads=[("f_a", ai, 0), ("f_a", ai, 1)] + [("f_dg", di, jj) for jj in range(3)],
                                      writes=[psk[pcv]])
                                gi = GEL.next()
                                gel = GEL.t[gi]
                                tr.op('act', lambda gel=gel, pcv=pcv, m=m: nc.scalar.activation(
                                    out=gel[:, 0:N], in_=ps[pcv][:, 0:N], func=AF.Gelu, bias=vcol(f"fcb{l}", m), scale=1.0),
                                    reads=[psk[pcv], "vecs"], writes=[("f_gel", gi)])
                                tr.op('dve', lambda gel=gel, pbb=pbb, m=m: nc.vector.tensor_tensor(
                                    out=gv[:, m, 0:N], in0=ps[pbb][:, 0:N], in1=gel[:, 0:N], op=ALU.mult),
                                    reads=[psk[pbb], ("f_gel", gi)], writes=[("f_gT", m)])
                        if ti + 1 < len(tl_all):
                            load_h2(ti + 1)
                        if pend_epi[0] is not None:
                            pend_epi[0]()
                            pend_epi[0] = None
                        for j in range(nb):
                            tr.dma('sp', XP.t[j][:], XM[(g0 + j) * 128:(g0 + j + 1) * 128, :], XP.sem[j],
                                   reads=[("XM", g0 + j)], writes=[("f_xp", j)])
                            tr.op('act', lambda j=j: nc.scalar.mul(out=XP.t[j][:], in_=XP.t[j][:], mul=float(ALPHA)),
                                  reads=[("f_xp", j)], writes=[("f_xp", j)])
                        kranges = [(0, 11), (11, 22), (22, 33), (33, 43)]
                        for c in range(4):
                            banks = [4 * (c % 2) + j for j in range(nb)]
                            for kg, (k0, k1) in enumerate(kranges):
                                w = wst.get()
                                wv = W.t[w][:, 0:5632].rearrange("p (k c) -> p k c", c=512)
                                kw = ("f_w", w)
                                for j in range(nb):
                                    def fd(j=j, wv=wv, k0=k0, k1=k1, kg=kg):
                                        for k in range(k0, k1):
                                            ins = nc.tensor.matmul(ps[banks[j]][:, :], lhsT=gv[:, k, j * 128:(j + 1) * 128],
                                                                   rhs=wv[:, k - k0, :], start=(k == 0), stop=(k == NFC - 1))
                                        return ins
                                    tr.op('pe', fd, reads=[kw] + [("f_gT", k) for k in range(k0, k1)], writes=[psk[banks[j]]])
                            for j in range(nb):
                                i1 = T1.next()
                                tr.op('dve', lambda i1=i1, j=j, c=c: nc.vector.tensor_tensor(
                                    out=T1.t[i1][:, :], in0=ps[banks[j]][:, :], in1=g2B[:, c * 512:(c + 1) * 512], op=ALU.mult),
                                    reads=[psk[banks[j]], "f_g2B"], writes=[("f_t1", i1)])
                                xp = XP.t[j]
                                tr.op('dve', lambda i1=i1, xp=xp, c=c: nc.vector.tensor_tensor(
                                    out=xp[:, c * 512:(c + 1) * 512], in0=xp[:, c * 512:(c + 1) * 512],
                                    in1=T1.t[i1][:, :], op=ALU.add),
                                    reads=[("f_t1", i1), ("f_xp", j)], writes=[("f_xp", j)])
                        def do_epi(g0=g0, nb=nb, sidx=sidx, base=base):
                            for j in range(nb):
                                g = g0 + j
                                if last:
                                    ob = (g - base - HALO) + (0 if sidx == 0 else 32)
                                    epi.run(XP.t[j][:], ("f_xp", j), XP.sem[j], g, yout[ob * 128:(ob + 1) * 128, :], None)
                                else:
                                    epi.run(XP.t[j][:], ("f_xp", j), XP.sem[j], g, X1[g * 128:(g + 1) * 128, :],
                                            H1T[:, :, 1 + g * 128:1 + (g + 1) * 128], S=S_of(l + 1, 0, sidx),
                                            T=T_of(l + 1, 0, sidx), xkey=("X1", g), hkey=("H1T", g))
                        pend_epi[0] = do_epi
                if pend_epi[0] is not None:
                    pend_epi[0]()
                phase_end()
                tr.free_sem(hsem)
                tr.free_sem(gsem)

        phases = [("pre", phase_modpre)]
        for l in range(DEPTH):
            phases += [(f"p1_{l}", lambda l=l: phase_p1(l)), (f"p2_{l}", lambda l=l: phase_p2(l)),
                       (f"p3_{l}", lambda l=l: phase_p3(l)), (f"p4_{l}", lambda l=l: phase_p4(l)),
                       (f"p5_{l}", lambda l=l: phase_p5(l))]
        for name, fn in phases:
            fn()
            if stop_after == name:
                break
        tr.barrier()
        print(f"[build] instructions={tr.nins} waits={tr.nwait}")
    return nc


_CACHE = {}


def make_in_maps(inp):
    inp = {k: np.asarray(v, dtype=np.float32) for k, v in inp.items()}
    wts = host_weights(inp)
    shared_cols, rows, U, kp, ident, sel = host_shared_tables(inp)
    in_maps = []
    for core in range(8):
        xin, vecs, qm = host_core_tables(inp, core, shared_cols)
        m = {"xin": xin, "vecs": vecs, "rows": rows, "utab": U, "qmask": qm, "kpat": kp, "ident": ident, "sel": sel}
        m.update(wts)
        in_maps.append(m)
    return in_maps


def kernel(**inputs):
    in_maps = make_in_maps(inputs)
    if "nc" not in _CACHE:
        _CACHE["nc"] = build()
    res = run_bass_kernel_spmd(_CACHE["nc"], in_maps, core_ids=list(range(8)))
    yp = np.zeros((2, 16384, D), np.float32)
    ys = np.zeros((4, 2048, D), np.float32)
    for core in range(8):
        y = res.results[core]["yout"]
        pb, pj = core // 4, core % 4
        sbi, sj = core // 2, core % 2
        yp[pb, pj * 4096:(pj + 1) * 4096] = y[0:4096]
        ys[sbi, sj * 1024:(sj + 1) * 1024] = y[4096:5120]
    return (yp, ys)
```
